# Optimizing a Trainium2 kernel written in Bass

```python
import math
import jax, jax.numpy as jnp
from jax import lax
import numpy as np

D_MODEL = 2048
BATCH = 4
SEQ = 4096
DEPTH = 2

N_MIXERS = 2
N_MLSTM = (DEPTH + 1) // 2
N_GMLP = DEPTH // 2
PROJ_FACTOR = 2
MLSTM_INNER = PROJ_FACTOR * D_MODEL
MLSTM_HEADS = 8
MLSTM_VD = MLSTM_INNER // MLSTM_HEADS
MLSTM_QKD = MLSTM_VD // 2
MLSTM_QK = MLSTM_HEADS * MLSTM_QKD
MLSTM_CHUNK = 64
CONV_K = 4
GMLP_INNER = PROJ_FACTOR * D_MODEL
GMLP_GROUPS = 8
GMLP_GD = GMLP_INNER // GMLP_GROUPS
GMLP_CHUNK = 128
EPS = 1e-6
MLSTM_IN_COLS = 2 * MLSTM_QK + 3 * MLSTM_INNER + 2 * MLSTM_HEADS
GMLP_IN_COLS = 3 * GMLP_INNER

kernel_name = "hybrid_mlstm_gmlp_trunk"


def rmsnorm(x, g):
    xf = x.astype(jnp.float32)
    y = xf * lax.rsqrt(jnp.mean(xf * xf, axis=-1, keepdims=True) + EPS)
    return (y * g.astype(jnp.float32)).astype(x.dtype)


def causal_depthwise_conv(x, w, b):
    k = w.shape[0]
    s = x.shape[1]
    xp = jnp.pad(x, ((0, 0), (k - 1, 0), (0, 0)))
    out = b
    for j in range(k):
        out = out + w[j] * xp[:, j:j + s]
    return out


def _mlstm_chunk(carry, inp):
    c_st, n_st, m_st = carry
    q, k, v, ig, lf = inp
    L = q.shape[2]
    b = jnp.cumsum(lf, axis=-1)
    g = b[..., -1]
    causal = jnp.tril(jnp.ones((L, L), dtype=bool))
    dmat = jnp.where(causal, b[..., :, None] - b[..., None, :] + ig[..., None, :], -jnp.inf)
    inter = b + m_st[..., None]
    m_t = jnp.maximum(inter, jnp.max(dmat, axis=-1))
    s = jnp.einsum('bhtd,bhsd->bhts', q, k) * jnp.exp(dmat - m_t[..., None])
    a = jnp.exp(inter - m_t)
    num = a[..., None] * jnp.einsum('bhtd,bhde->bhte', q, c_st) + jnp.einsum('bhts,bhse->bhte', s, v)
    den = a * jnp.einsum('bhtd,bhd->bht', q, n_st) + jnp.sum(s, axis=-1)
    h = num / jnp.maximum(jnp.abs(den), jnp.exp(-m_t))[..., None]
    w = g[..., None] - b + ig
    m_new = jnp.maximum(g + m_st, jnp.max(w, axis=-1))
    wexp = jnp.exp(w - m_new[..., None])
    decay = jnp.exp(g + m_st - m_new)
    c_new = decay[..., None, None] * c_st + jnp.einsum('bhsd,bhse->bhde', k * wexp[..., None], v)
    n_new = decay[..., None] * n_st + jnp.einsum('bhs,bhsd->bhd', wexp, k)
    return (c_new, n_new, m_new), h


def mlstm_cell(q, k, v, ig, lf):
    bsz, s, h, dk = q.shape
    dv = v.shape[-1]
    nc = s // MLSTM_CHUNK

    def to_chunks(t):
        t = t.reshape((bsz, nc, MLSTM_CHUNK, h) + t.shape[3:])
        return jnp.moveaxis(jnp.moveaxis(t, 1, 0), 3, 2)

    init = (jnp.zeros((bsz, h, dk, dv), jnp.float32),
            jnp.zeros((bsz, h, dk), jnp.float32),
            jnp.zeros((bsz, h), jnp.float32))
    _, hs = lax.scan(_mlstm_chunk, init,
                     (to_chunks(q), to_chunks(k), to_chunks(v), to_chunks(ig), to_chunks(lf)))
    return jnp.transpose(hs, (1, 0, 3, 2, 4)).reshape(bsz, s, h, dv)


def mlstm_layer(xn, w_in, conv_w, conv_b, gate_b, head_g, w_out):
    bsz, s, _ = xn.shape
    p = xn @ w_in
    o1 = 2 * MLSTM_QK
    o2 = o1 + MLSTM_INNER
    o3 = o2 + MLSTM_INNER
    o4 = o3 + MLSTM_INNER
    qk = jax.nn.silu(causal_depthwise_conv(p[..., :o1], conv_w, conv_b))
    q = qk[..., :MLSTM_QK].reshape(bsz, s, MLSTM_HEADS, MLSTM_QKD)
    k = qk[..., MLSTM_QK:].reshape(bsz, s, MLSTM_HEADS, MLSTM_QKD)
    v = p[..., o1:o2].reshape(bsz, s, MLSTM_HEADS, MLSTM_VD)
    o = p[..., o2:o3].reshape(bsz, s, MLSTM_HEADS, MLSTM_VD)
    z = p[..., o3:o4]
    gates = (p[..., o4:] + gate_b).astype(jnp.float32)
    ig = gates[..., :MLSTM_HEADS]
    lf = jax.nn.log_sigmoid(gates[..., MLSTM_HEADS:])
    f32 = jnp.float32
    hc = mlstm_cell(q.astype(f32) * (MLSTM_QKD ** -0.5), k.astype(f32), v.astype(f32), ig, lf)
    hc = hc * jax.nn.sigmoid(o.astype(f32))
    hc = hc * lax.rsqrt(jnp.mean(hc * hc, axis=-1, keepdims=True) + EPS)
    hc = hc.reshape(bsz, s, MLSTM_INNER) * head_g.astype(f32)
    y = hc.astype(xn.dtype) * jax.nn.silu(z)
    return y @ w_out


def gmlp_layer(xn, w_in, ln_g, ln_b, w_s, b_s, w_out):
    bsz, s, _ = xn.shape
    p = xn @ w_in
    u = jax.nn.gelu(p[..., :GMLP_INNER], approximate=False)
    v = jax.nn.gelu(p[..., GMLP_INNER:2 * GMLP_INNER], approximate=False)
    z = p[..., 2 * GMLP_INNER:]
    vf = v.astype(jnp.float32)
    mu = jnp.mean(vf, axis=-1, keepdims=True)
    var = jnp.mean(jnp.square(vf - mu), axis=-1, keepdims=True)
    vf = (vf - mu) * lax.rsqrt(var + EPS) * ln_g.astype(jnp.float32) + ln_b.astype(jnp.float32)
    v = vf.astype(xn.dtype)
    nch = s // GMLP_CHUNK
    v = v.reshape(bsz, nch, GMLP_CHUNK, GMLP_GROUPS, GMLP_GD)
    mask = jnp.tril(jnp.ones((GMLP_CHUNK, GMLP_CHUNK), dtype=w_s.dtype))
    ws = w_s * mask
    sv = jnp.einsum('gts,bcsgd->bctgd', ws, v) + jnp.transpose(b_s)[None, None, :, :, None]
    sv = sv.reshape(bsz, s, GMLP_INNER)
    y = u * sv * jax.nn.silu(z)
    return y @ w_out


def setup_inputs(seed: int = 0) -> dict:
    key = jax.random.key(seed)
    ks = jax.random.split(key, 20)
    f32 = jnp.float32
    nrm = lambda k, shp, sc: jax.random.normal(k, shp, f32) * sc
    x = jax.random.normal(ks[0], (BATCH, SEQ, D_MODEL), f32)
    mlstm_norm_g = 1.0 + nrm(ks[1], (N_MLSTM, D_MODEL), 0.02)
    mlstm_w_in = nrm(ks[2], (N_MLSTM, D_MODEL, MLSTM_IN_COLS), D_MODEL ** -0.5)
    mlstm_conv_w = nrm(ks[3], (N_MLSTM, CONV_K, 2 * MLSTM_QK), CONV_K ** -0.5)
    mlstm_conv_b = nrm(ks[4], (N_MLSTM, 2 * MLSTM_QK), 0.01)
    ig_b = nrm(ks[5], (N_MLSTM, MLSTM_HEADS), 0.1)
    fg_b = 3.0 + nrm(ks[6], (N_MLSTM, MLSTM_HEADS), 0.5)
    mlstm_gate_b = jnp.concatenate([ig_b, fg_b], axis=-1)
    mlstm_head_g = 1.0 + nrm(ks[7], (N_MLSTM, MLSTM_INNER), 0.02)
    mlstm_w_out = nrm(ks[8], (N_MLSTM, MLSTM_INNER, D_MODEL), MLSTM_INNER ** -0.5)
    gmlp_norm_g = 1.0 + nrm(ks[9], (N_GMLP, D_MODEL), 0.02)
    gmlp_w_in = nrm(ks[10], (N_GMLP, D_MODEL, GMLP_IN_COLS), D_MODEL ** -0.5)
    gmlp_ln_g = 1.0 + nrm(ks[11], (N_GMLP, GMLP_INNER), 0.02)
    gmlp_ln_b = nrm(ks[12], (N_GMLP, GMLP_INNER), 0.01)
    gmlp_w_s = nrm(ks[13], (N_GMLP, GMLP_GROUPS, GMLP_CHUNK, GMLP_CHUNK), GMLP_CHUNK ** -0.5)
    gmlp_b_s = 1.0 + nrm(ks[14], (N_GMLP, GMLP_GROUPS, GMLP_CHUNK), 0.01)
    gmlp_w_out = nrm(ks[15], (N_GMLP, GMLP_INNER, D_MODEL), GMLP_INNER ** -0.5)
    final_norm_g = 1.0 + nrm(ks[16], (D_MODEL,), 0.02)
    return {"x": x,
            "mlstm_norm_g": mlstm_norm_g, "mlstm_w_in": mlstm_w_in,
            "mlstm_conv_w": mlstm_conv_w, "mlstm_conv_b": mlstm_conv_b,
            "mlstm_gate_b": mlstm_gate_b, "mlstm_head_g": mlstm_head_g,
            "mlstm_w_out": mlstm_w_out,
            "gmlp_norm_g": gmlp_norm_g, "gmlp_w_in": gmlp_w_in,
            "gmlp_ln_g": gmlp_ln_g, "gmlp_ln_b": gmlp_ln_b,
            "gmlp_w_s": gmlp_w_s, "gmlp_b_s": gmlp_b_s, "gmlp_w_out": gmlp_w_out,
            "final_norm_g": final_norm_g}


def reference(x, mlstm_norm_g, mlstm_w_in, mlstm_conv_w, mlstm_conv_b, mlstm_gate_b,
              mlstm_head_g, mlstm_w_out, gmlp_norm_g, gmlp_w_in, gmlp_ln_g, gmlp_ln_b,
              gmlp_w_s, gmlp_b_s, gmlp_w_out, final_norm_g):
    h = x
    for i in range(DEPTH):
        j = i // N_MIXERS
        if i % N_MIXERS == 0:
            xn = rmsnorm(h, mlstm_norm_g[j])
            h = h + mlstm_layer(xn, mlstm_w_in[j], mlstm_conv_w[j], mlstm_conv_b[j],
                                mlstm_gate_b[j], mlstm_head_g[j], mlstm_w_out[j])
        else:
            xn = rmsnorm(h, gmlp_norm_g[j])
            h = h + gmlp_layer(xn, gmlp_w_in[j], gmlp_ln_g[j], gmlp_ln_b[j],
                               gmlp_w_s[j], gmlp_b_s[j], gmlp_w_out[j])
    return rmsnorm(h, final_norm_g)
```

```python
import numpy as np
from contextlib import ExitStack
import concourse.bass as bass
import concourse.mybir as mybir
from concourse.bass_utils import run_bass_kernel_spmd

F32 = mybir.dt.float32
BF16 = mybir.dt.bfloat16
AF = mybir.ActivationFunctionType
ALU = mybir.AluOpType

D = 2048
NTOK = 2048
T = 512
NSUB = T // 128
NT = NTOK // T
KC = D // 128
H = 8
EPS = 1e-6
IN0 = 16400
IN1 = 12288
SAME_ENGINE_SYNC = True
NWSLOT = 3


class Buf:
    __slots__ = ("w", "r", "name")

    def __init__(self, name=""):
        self.w = None
        self.r = {}
        self.name = name


class DSem:
    def __init__(self, sem, key):
        self.sem = sem
        self.cnt = 0
        self.key = key


class TK:
    def __init__(self, nc, es):
        self.nc = nc
        self.es = es
        self.eng = {"pe": nc.tensor, "act": nc.scalar, "dve": nc.vector,
                    "pool": nc.gpsimd, "sp": nc.sync}
        self.sem = {k: es.enter_context(nc.semaphore("s_" + k)) for k in self.eng}
        self.cnt = {k: 0 for k in self.eng}
        self.seen = {k: {} for k in self.eng}
        self.ndsem = 0

    def dsem(self):
        self.ndsem += 1
        key = "d%d" % self.ndsem
        return DSem(self.es.enter_context(self.nc.semaphore(key)), key)

    def wait(self, e, tk):
        if tk is None:
            return
        sem, val, key = tk
        if key == e and (e == "pe" or not SAME_ENGINE_SYNC):
            return
        if self.seen[e].get(key, 0) >= val:
            return
        self.eng[e].wait_ge(sem, val)
        self.seen[e][key] = val

    def _deps(self, e, reads, writes):
        for b in reads:
            self.wait(e, b.w)
        for b in writes:
            self.wait(e, b.w)
            for t in b.r.values():
                self.wait(e, t)

    def _mark(self, tk, reads, writes):
        for b in reads:
            b.r[tk[2]] = tk
        for b in writes:
            b.w = tk
            b.r = {}

    def op(self, e, fn, reads=(), writes=()):
        self._deps(e, reads, writes)
        ins = fn(self.eng[e])
        self.cnt[e] += 1
        ins.then_inc(self.sem[e], 1)
        tk = (self.sem[e], self.cnt[e], e)
        self._mark(tk, reads, writes)
        return tk

    def mm(self, fns, reads=(), writes=()):
        e = "pe"
        self._deps(e, reads, writes)
        ins = None
        for fn in fns:
            ins = fn(self.eng[e])
        self.cnt[e] += 1
        ins.then_inc(self.sem[e], 1)
        tk = (self.sem[e], self.cnt[e], e)
        self._mark(tk, reads, writes)
        return tk

    def dma(self, e, ds, out, in_, reads=(), writes=()):
        self._deps(e, reads, writes)
        self.eng[e].dma_start(out=out, in_=in_).then_inc(ds.sem, 16)
        ds.cnt += 16
        tk = (ds.sem, ds.cnt, ds.key)
        self._mark(tk, reads, writes)
        return tk

    def snapshot(self):
        return [(self.sem[e], self.cnt[e], e) for e in ("pe", "act", "dve") if self.cnt[e] > 0]

    def barrier(self, engines=("pe", "act", "dve", "sp"), tks=None):
        if tks is None:
            tks = self.snapshot()
        for f in engines:
            for tk in tks:
                if tk[2] != f:
                    self.wait(f, tk)


class WQ:
    def __init__(self, tk, slots, tiles):
        self.tk = tk
        self.slots = slots
        self.tiles = tiles
        self.nl = 0
        self.nu = 0

    def _load(self):
        i = self.nl
        if i >= len(self.tiles):
            return
        t, b, ds = self.slots[i % len(self.slots)]
        tk = self.tk
        tk._deps("pool", (), [b])
        for (co, ncol, k0, nk, src) in self.tiles[i]:
            tk.eng["pool"].dma_start(out=t[:, k0:k0 + nk, co:co + ncol], in_=src).then_inc(ds.sem, 16)
            ds.cnt += 16
        tk._mark((ds.sem, ds.cnt, ds.key), (), [b])
        self.nl += 1

    def prime(self):
        for _ in self.slots:
            self._load()

    def cur(self):
        assert self.nu < len(self.tiles), "weight stream exhausted"
        t, b, ds = self.slots[self.nu % len(self.slots)]
        return t, b

    def peek(self, k):
        assert self.nu + k < self.nl, "tile not prefetched yet"
        t, b, ds = self.slots[(self.nu + k) % len(self.slots)]
        return t, b

    def done(self):
        self.nu += 1
        self._load()


def build_program():
    nc = bass.Bass("TRN2", target_bir_lowering=False)
    dt = lambda n, s, k="ExternalInput": nc.dram_tensor(n, s, F32, kind=k).ap()
    xm = dt("xm", [NTOK, D])
    xp = dt("xp", [NTOK, D])
    flag_d = dt("flag", [128, 1])
    w_in0 = dt("w_in0", [D, IN0])
    w_out0 = dt("w_out0", [4096, D])
    w_in1 = dt("w_in1", [D, IN1])
    w_out1 = dt("w_out1", [4096, D])
    g0bc_d = dt("g0bc", [128, D])
    g1bc_d = dt("g1bc", [128, D])
    gfbc_d = dt("gfbc", [128, D])
    convw_d = dt("convw", [128, 32 * 4])
    convb_d = dt("convb", [128, 32])
    gateb_d = dt("gateb", [128, 16])
    hgbc_d = dt("hgbc", [128, 4096])
    lng_d = dt("lng", [128, 32])
    lnb_d = dt("lnb", [128, 32])
    wsT_d = dt("wsT", [128, 8 * 128])
    bsbc_d = dt("bsbc", [128, 8 * 128])
    maskT_d = dt("maskT", [128, 128])
    ident_d = dt("ident", [128, 128])
    out_d = dt("out", [NTOK, D], "ExternalOutput")

    w_in0v = w_in0.rearrange("(kc p) c -> p kc c", p=128)
    w_in1v = w_in1.rearrange("(kc p) c -> p kc c", p=128)
    w_out0v = w_out0.rearrange("(kc p) c -> p kc c", p=128)
    w_out1v = w_out1.rearrange("(kc p) c -> p kc c", p=128)

    with ExitStack() as es:
        tk = TK(nc, es)
        sb = lambda n, s, d: es.enter_context(nc.sbuf_tensor("sb_" + n, s, d))
        ps = lambda n, s, d: es.enter_context(nc.psum_tensor("ps_" + n, s, d))

        Hs = sb("Hs", [128, NSUB, D], F32)
        Hb = [Buf("H%d" % i) for i in range(NSUB)]
        xnT = sb("xnT", [128, KC, T], BF16)
        xnTb = [Buf("xnT%d" % i) for i in range(NSUB)]
        YT = sb("YT", [128, 32, T], BF16)
        YTb = [Buf("YT%d" % i) for i in range(32)]
        Cst = sb("Cst", [128, H, 2, 512], F32)
        Cstb = [[Buf("C%d_%d" % (i, j)) for j in range(2)] for i in range(H)]
        nst = sb("nst", [128, H, 2], F32)
        nstb = [Buf("n%d" % i) for i in range(H)]
        wslots = []
        for i in range(NWSLOT):
            wslots.append((sb("w%d" % i, [128, KC, 512], BF16), Buf("w%d" % i), tk.dsem()))
        AB = sb("AB", [128, 12544], BF16)
        AFt = sb("AFt", [128, 1540], F32)
        xnb = AB[:, 6144:8192]
        xnbb = Buf("xnb")
        wg = sb("wg", [128, KC, 16], BF16); wgb = Buf()
        flag = sb("flagt", [128, 1], F32); flagb = Buf()
        convw = sb("convw", [128, 32, 4], F32); convwb = Buf()
        convb = sb("convb", [128, 32], F32); convbb = Buf()
        gateb = sb("gateb", [128, 16], F32); gatebb = Buf()
        lng = sb("lng", [128, 32], F32); lngb = Buf()
        lnb = sb("lnb", [128, 32], F32); lnbb = Buf()
        wsf = AFt[:, 0:1024].rearrange("p (g t) -> p g t", g=8); wsfb = Buf()
        wsm = sb("wsm", [128, 8, 128], BF16); wsmb = Buf()
        bsbc = sb("bsbc", [128, 8, 128], F32); bsbcb = Buf()
        rsbc = sb("rsbc", [128, 8, 128], F32); rsbcb = Buf()
        maskT = sb("maskT", [128, 128], F32); maskTb = Buf()
        identf = AFt[:, 1028:1156]; identfb = Buf()
        ident = sb("ident", [128, 128], BF16); identb = Buf()
        onesf = sb("onesf", [128, 128], F32); onesfb = Buf()
        onesb = sb("onesb", [128, 2], BF16); onesbb = Buf()
        hgh1 = sb("hgh", [128, 512], F32)
        hgh = [hgh1, hgh1]
        hghb1 = Buf()
        hghb = [hghb1, hghb1]
        hghs1 = tk.dsem()
        hghs = [hghs1, hghs1]
        halo = sb("halo", [128, 32, 4], F32); halob = [Buf() for _ in range(32)]
        gI = sb("gI", [128, NSUB, 8], F32); gIb = Buf()
        gZ = sb("gZ", [128, NSUB, 8], F32); gZb = Buf()
        gA = sb("gA", [128, NSUB, 8], F32); gAb = Buf()
        gL = sb("gL", [128, NSUB, 8], F32); gLb = Buf()
        beta = sb("beta", [128, NSUB, 8], F32); betab = Buf()
        beta16 = sb("beta16", [128, NSUB, 8], F32); beta16b = Buf()
        emb = sb("emb", [128, NSUB, 8], F32); embb = Buf()
        egt = sb("egt", [128, NSUB, 8], F32); egb = Buf()
        eg16t = sb("eg16t", [128, NSUB, 8], F32)
        sm = sb("sm", [128, 64], F32)
        smb = [Buf() for _ in range(64)]
        nbt = sb("nbt", [128, 2, 2], BF16); nbb = [Buf(), Buf()]
        st1 = sb("st1", [128, NSUB, 8], F32); st1b = Buf()
        st2 = sb("st2", [128, NSUB, 8], F32); st2b = Buf()
        lnm = sb("lnm", [128, NSUB], F32); lnmb = Buf()
        lnr = sb("lnr", [128, NSUB], F32); lnrb = Buf()
        lnt = sb("lnt", [128, NSUB], F32); lntb = Buf()

        def ab(o, n):
            return AB[:, o:o + n]
        qT = ab(0, 1024).rearrange("p (c t) -> p c t", c=2); qTb = Buf("qT")
        kT = ab(1024, 1024).rearrange("p (c t) -> p c t", c=2); kTb = Buf("kT")
        kp = ab(2048, 1024).rearrange("p (c d) -> p c d", c=4); kpb = Buf("kp")
        vt = ab(3072, 2048).rearrange("p (c e) -> p c e", c=4); vtb = [Buf("v%d" % i) for i in range(4)]
        so = ab(5120, 2048).rearrange("p (c e) -> p c e", c=4); sob = [Buf("so%d" % i) for i in range(4)]
        sz = ab(7168, 2048).rearrange("p (c e) -> p c e", c=4); szb = Buf("sz")
        STt = [ab(9216, 128), ab(9344, 128)]; STb = [Buf(), Buf()]
        ybf = [ab(9472, 512), ab(9984, 512)]; ybfb = [Buf(), Buf()]
        Cb = [ab(10496, 1024).rearrange("p (c e) -> p c e", c=2),
              ab(11520, 1024).rearrange("p (c e) -> p c e", c=2)]
        Cbb = [Buf(), Buf()]
        cbuf = AFt[:, 0:516]; cbufb = Buf("cbuf")
        acc = AFt[:, 516:1028]; accb = Buf("acc")
        hc = AFt[:, 1028:1540]; hcb = Buf("hc")
        gbc = AB[:, 8192:12288].bitcast(F32); gbcb = Buf("gbc"); gbcs = tk.dsem()
        xnb2 = ab(4096, 2048); xnb2b = Buf("xnb2")
        junkN = ab(0, 2048); junkNb = Buf("junkN")
        uT = ab(0, 2048).rearrange("p (c t) -> p c t", c=4); uTb = [Buf() for _ in range(4)]
        szT = [ab(2048, 512), ab(2560, 512)]; szTb = [Buf(), Buf()]
        junk1 = ab(3072, 512); junk1b = Buf()
        sv = [AFt[:, 0:512], AFt[:, 512:1024]]; svb = [Buf(), Buf()]

        PJ = [ps("PJ0", [128, 512], F32), ps("PJ1", [128, 512], F32)]
        PJb = [Buf("PJ0"), Buf("PJ1")]
        PT = ps("PT", [128, 8, 128], BF16); PTb = Buf("PT")
        PS_ = ps("PS", [128, 512], F32); PSb = Buf("PS")
        PN = ps("PN", [128, 512], F32); PNb = Buf("PN")
        PD = ps("PD", [128, 512], F32); PDb = Buf("PD"); PDnb = Buf("PDn")
        PK = [ps("PK0", [128, 512], F32), ps("PK1", [128, 512], F32)]
        PKb = [Buf("PK0"), Buf("PK1")]
        PT2 = PK[0][:, :].bitcast(BF16).rearrange("p (j d) -> p j d", j=8)
        PT3 = PK[1][:, :].bitcast(BF16).rearrange("p (j d) -> p j d", j=8)
        PT4 = PD[:, :].bitcast(BF16).rearrange("p (j d) -> p j d", j=8)
        pjc = [0]

        def next_pj():
            i = pjc[0] % 2
            pjc[0] += 1
            return PJ[i], PJb[i]

        tiles = []

        def qk_tile(h):
            return [(0, 256, 0, KC, w_in0v[:, :, h * 256:(h + 1) * 256]),
                    (256, 256, 0, KC, w_in0v[:, :, 2048 + h * 256:2048 + (h + 1) * 256])]

        def col_tile(wv, c0, n=512):
            return [(0, n, 0, KC, wv[:, :, c0:c0 + n])]

        def wout_tiles(wv):
            r = []
            for cb in range(4):
                for half in range(2):
                    r.append([(0, 512, 0, KC, wv[:, half * KC:(half + 1) * KC, cb * 512:(cb + 1) * 512])])
            return r

        for _pt in range(NT):
            for h in range(H):
                qk_ = qk_tile(h) if _pt == NT - 1 else qk_tile(h)[1:]
                v_ = col_tile(w_in0v, 4096 + h * 512)
                tiles += [v_, qk_] if h == 0 else [qk_, v_]
        for _mt in range(NT):
            for h in range(H):
                qk_ = qk_tile(h)
                v_ = col_tile(w_in0v, 4096 + h * 512)
                tiles += [v_, qk_] if h == 0 else [qk_, v_]
                tiles.append(col_tile(w_in0v, 8192 + h * 512))
                tiles.append(col_tile(w_in0v, 12288 + h * 512))
            tiles += wout_tiles(w_out0v)
            for vb in range(8):
                tiles.append(col_tile(w_in1v, 4096 + vb * 512))
            for g in range(8):
                tiles.append(col_tile(w_in1v, g * 512))
                tiles.append(col_tile(w_in1v, 8192 + g * 512))
            tiles += wout_tiles(w_out1v)
        wq = WQ(tk, wslots, tiles)

        cs = tk.dsem()

        cbufs = []

        def cload(t_ap, d_ap, b, e="sp"):
            tk.dma(e, cs, t_ap, d_ap, writes=[b])
            cbufs.append(b)

        cload(flag[:], flag_d, flagb)
        cload(convw[:], convw_d.rearrange("p (c j) -> p c j", j=4), convwb)
        cload(convb[:], convb_d, convbb)
        cload(gateb[:], gateb_d, gatebb)
        cload(lng[:], lng_d, lngb)
        cload(lnb[:], lnb_d, lnbb)
        cload(wsf, wsT_d.rearrange("p (g t) -> p g t", g=8), wsfb)
        cbufs += [cbufb, accb, hcb]
        cload(bsbc[:], bsbc_d.rearrange("p (g t) -> p g t", g=8), bsbcb)
        cload(maskT[:], maskT_d, maskTb)
        cload(identf, ident_d, identfb)
        for b_ in cbufs:
            b_.w = (cs.sem, cs.cnt, cs.key)
        wgs = tk.dsem()
        tk.dma("pool", wgs, wg[:], w_in0v[:, :, 16384:16400], writes=[wgb])
        wq.prime()

        tk.op("dve", lambda v: v.memset(onesf[:], 1.0), writes=[onesfb])
        tk.op("dve", lambda v: v.memset(onesb[:], 1.0), writes=[onesbb])
        tk.op("dve", lambda v: v.memset(halo[:], 0.0), writes=halob)
        tk.op("dve", lambda v: v.memset(Cst[:], 0.0), writes=[b for hb in Cstb for b in hb])
        tk.op("dve", lambda v: v.memset(nst[:], 0.0), writes=nstb)
        tk.op("act", lambda a: a.activation(out=ident[:], in_=identf, func=AF.Copy),
              reads=[identfb, hcb], writes=[identb])
        tk.op("dve", lambda v: v.tensor_tensor(out=wsm[:], in0=wsf,
                                               in1=maskT[:].unsqueeze(1).to_broadcast([128, 8, 128]),
                                               op=ALU.mult),
              reads=[wsfb, maskTb, cbufb, accb], writes=[wsmb])
        tk.op("dve", lambda v: v.tensor_tensor(out=wsf, in0=wsf,
                                               in1=maskT[:].unsqueeze(1).to_broadcast([128, 8, 128]),
                                               op=ALU.mult),
              reads=[maskTb], writes=[wsfb, cbufb, accb])
        for half in range(2):
            tk.mm([lambda p, half=half: p.matmul(PN[:, :], lhsT=onesf[:],
                                                 rhs=wsf[:, half * 4:(half + 1) * 4, :].rearrange("p g t -> p (g t)"),
                                                 start=True, stop=True)],
                  reads=[onesfb, wsfb, cbufb, accb], writes=[PNb])
            tk.op("act", lambda a, half=half: a.activation(
                out=rsbc[:, half * 4:(half + 1) * 4, :].rearrange("p g t -> p (g t)"), in_=PN[:, :], func=AF.Copy),
                reads=[PNb], writes=[rsbcb])

        gain_res = [None]

        def ensure_gain(gain_d):
            if gain_res[0] is gain_d:
                return
            tk.dma("sp", gbcs, gbc, gain_d, writes=[gbcb, szb] + STb + ybfb + Cbb)
            gain_res[0] = gain_d

        def rmsnorm_to_xnT(gain_d, after_sub=None):
            ensure_gain(gain_d)
            tk.op("dve", lambda v: v.memset(sm[:, 0:4], 0.0), writes=smb[0:4])

            def sq(s):
                tk.op("act", lambda a: a.activation(out=junkN, in_=Hs[:, s, :], func=AF.Square, accum_out=sm[:, s:s + 1]),
                      reads=[Hb[s]], writes=[junkNb, smb[s]])

            def lnexp(s):
                tk.op("act", lambda a: a.activation(out=sm[:, 8 + s:9 + s], in_=sm[:, s:s + 1], func=AF.Ln,
                                                    scale=1.0 / D, bias=EPS), reads=[smb[s]], writes=[smb[8 + s]])
                tk.op("act", lambda a: a.activation(out=sm[:, 16 + s:17 + s], in_=sm[:, 8 + s:9 + s], func=AF.Exp,
                                                    scale=-0.5), reads=[smb[8 + s]], writes=[smb[16 + s]])

            sq(0); lnexp(0); sq(1); lnexp(1)
            xb2 = [(xnb, xnbb), (xnb2, xnb2b)]
            ptv = [(PT[:, :, :], PTb), (PT2, PKb[0]), (PT3, PKb[1]), (PT4, PDb)]

            def scale(s):
                xb, xbb = xb2[s % 2]
                tk.op("dve", lambda v: v.scalar_tensor_tensor(out=xb, in0=Hs[:, s, :], scalar=sm[:, 16 + s:17 + s],
                                                              in1=gbc, op0=ALU.mult, op1=ALU.mult),
                      reads=[Hb[s], smb[16 + s], gbcb], writes=[xbb])

            scale(0)
            for s in range(NSUB):
                xb, xbb = xb2[s % 2]
                if s + 1 < NSUB:
                    scale(s + 1)
                for half in range(2):
                    pt, ptb = ptv[(s % 2) * 2 + half]
                    tk.mm([lambda p, j=j, half=half, xb=xb, pt=pt: p.transpose(
                        out=pt[:, j, :], in_=xb[:, (half * 8 + j) * 128:(half * 8 + j + 1) * 128],
                        identity=ident[:]) for j in range(8)],
                        reads=[xbb, identb], writes=[ptb])
                    if half == 0:
                        tk.op("act", lambda a, s=s, half=half, pt=pt: a.activation(
                            out=xnT[:, half * 8:(half + 1) * 8, s * 128:(s + 1) * 128], in_=pt, func=AF.Copy),
                            reads=[ptb], writes=[xnTb[s]])
                    else:
                        tk.op("dve", lambda v, s=s, half=half, pt=pt: v.tensor_copy(
                            out=xnT[:, half * 8:(half + 1) * 8, s * 128:(s + 1) * 128], in_=pt),
                            reads=[ptb], writes=[xnTb[s]])
                if s + 2 < NSUB:
                    sq(s + 2); lnexp(s + 2)
                if after_sub is not None and s >= 1:
                    after_sub(s - 1)
            if after_sub is not None:
                after_sub(NSUB - 1)

        def proj_fm(wt, wb, c0, n=T, t0=0):
            pj, pjb = next_pj()
            tk.mm([lambda p, kc=kc: p.matmul(pj[:, 0:n], lhsT=wt[:, kc, c0:c0 + 128], rhs=xnT[:, kc, t0:t0 + n],
                                             start=(kc == 0), stop=(kc == KC - 1)) for kc in range(KC)],
                  reads=[wb] + xnTb, writes=[pjb])
            return pj, pjb

        def proj_tm(wt, wb, s, n=512, pj=None, pjb=None, start=True, stop=True, lhs=None, lhsb=None, kcs=KC):
            if pj is None:
                pj, pjb = next_pj()
            if lhs is None:
                fns = [lambda p, kc=kc: p.matmul(pj[:, 0:n], lhsT=xnT[:, kc, s * 128:(s + 1) * 128], rhs=wt[:, kc, 0:n],
                                                 start=(start and kc == 0), stop=(stop and kc == kcs - 1))
                       for kc in range(kcs)]
                rd = [wb, xnTb[s]]
            else:
                fns = [lambda p, kc=kc: p.matmul(pj[:, 0:n], lhsT=lhs(kc), rhs=wt[:, kc, 0:n],
                                                 start=(start and kc == 0), stop=(stop and kc == kcs - 1))
                       for kc in range(kcs)]
                rd = [wb] + lhsb
            tk.mm(fns, reads=rd, writes=[pjb])
            return pj, pjb

        def gates_a():
            tk.mm([lambda p, s=s, kc=kc: p.matmul(PS_[:, s * 16:(s + 1) * 16], lhsT=xnT[:, kc, s * 128:(s + 1) * 128],
                                                  rhs=wg[:, kc, :], start=(kc == 0), stop=(kc == KC - 1))
                   for s in range(NSUB) for kc in range(KC)],
                  reads=xnTb + [wgb], writes=[PSb])
            psv = PS_[:, 0:NSUB * 16].rearrange("p (s g) -> p s g", g=16)
            tk.op("dve", lambda v: v.tensor_tensor(out=gI[:], in0=psv[:, :, 0:8],
                                                   in1=gateb[:, 0:8].unsqueeze(1).to_broadcast([128, NSUB, 8]),
                                                   op=ALU.add), reads=[PSb, gatebb], writes=[gIb])
            tk.op("dve", lambda v: v.tensor_tensor(out=gZ[:], in0=psv[:, :, 8:16],
                                                   in1=gateb[:, 8:16].unsqueeze(1).to_broadcast([128, NSUB, 8]),
                                                   op=ALU.add), reads=[PSb, gatebb], writes=[gZb])
            tk.op("dve", lambda v: v.scalar_tensor_tensor(out=gA[:], in0=gZ[:], scalar=-1.0, in1=gZ[:],
                                                          op0=ALU.mult, op1=ALU.max), reads=[gZb], writes=[gAb])
            tk.op("act", lambda a: a.activation(out=gA[:], in_=gA[:], func=AF.Exp, scale=-1.0),
                  reads=[gAb], writes=[gAb])
            tk.op("act", lambda a: a.activation(out=gA[:], in_=gA[:], func=AF.Ln, bias=1.0),
                  reads=[gAb], writes=[gAb])
            tk.op("dve", lambda v: v.scalar_tensor_tensor(out=gL[:], in0=gZ[:], scalar=0.0, in1=gA[:],
                                                          op0=ALU.min, op1=ALU.subtract),
                  reads=[gZb, gAb], writes=[gLb])

        def gates_b():
            psv = PS_[:, 0:NSUB * 16].rearrange("p (s g) -> p s g", g=16)
            gLf = gL[:].rearrange("p s g -> p (s g)")
            tk.mm([lambda p: p.matmul(PS_[:, 64:96], lhsT=maskT[:], rhs=gLf, start=True, stop=True),
                   lambda p: p.matmul(PS_[:, 96:128], lhsT=onesf[:], rhs=gLf, start=True, stop=True)],
                  reads=[maskTb, onesfb, gLb], writes=[PSb])
            bps = PS_[:, 64:96].rearrange("p (s g) -> p s g", g=8)
            gps = PS_[:, 96:128].rearrange("p (s g) -> p s g", g=8)
            tk.op("dve", lambda v: v.tensor_tensor(out=beta[:], in0=gI[:], in1=bps, op=ALU.subtract),
                  reads=[gIb, PSb], writes=[betab])
            tk.op("act", lambda a: a.activation(out=beta[:], in_=beta[:], func=AF.Exp), reads=[betab], writes=[betab])
            tk.op("act", lambda a: a.activation(out=beta16[:], in_=beta[:], func=AF.Copy, scale=1.0 / 16),
                  reads=[betab], writes=[beta16b])
            tk.op("act", lambda a: a.activation(out=emb[:], in_=bps, func=AF.Exp, scale=-1.0),
                  reads=[PSb], writes=[embb])
            tk.op("act", lambda a: a.activation(out=egt[:], in_=gps, func=AF.Exp), reads=[PSb], writes=[egb])
            tk.op("act", lambda a: a.activation(out=eg16t[:], in_=egt[:], func=AF.Copy, scale=1.0 / 16), writes=[egb])

        def conv_chunk(wt, wb, c0, ch, dstT, dstb, cc, n=T):
            pj, pjb = proj_fm(wt, wb, c0)
            tk.op("act", lambda a: a.activation(out=cbuf[:, 0:3], in_=halo[:, ch, 0:3], func=AF.Copy),
                  reads=[halob[ch]], writes=[cbufb])
            tk.op("act", lambda a: a.activation(out=cbuf[:, 3:3 + T], in_=pj[:, 0:T], func=AF.Copy),
                  reads=[pjb], writes=[cbufb])
            tk.op("act", lambda a: a.activation(out=halo[:, ch, 0:3], in_=cbuf[:, T:T + 3], func=AF.Copy),
                  reads=[cbufb], writes=[halob[ch]])
            tk.op("dve", lambda v: v.tensor_scalar(out=acc, in0=cbuf[:, 3:3 + T], scalar1=convw[:, ch, 3:4],
                                                   scalar2=convb[:, ch:ch + 1], op0=ALU.mult, op1=ALU.add),
                  reads=[cbufb, convwb, convbb], writes=[accb])
            for j in (2, 1, 0):
                tk.op("dve", lambda v, j=j: v.scalar_tensor_tensor(out=acc, in0=cbuf[:, j:j + T],
                                                                    scalar=convw[:, ch, j:j + 1], in1=acc,
                                                                    op0=ALU.mult, op1=ALU.add),
                      reads=[cbufb, convwb], writes=[accb])
            tk.op("act", lambda a: a.activation(out=dstT[:, cc, :], in_=acc, func=AF.Silu),
                  reads=[accb], writes=[dstb])

        def k_transposes(h):
            tk.mm([lambda p, c=c, cc=cc: p.transpose(out=PT[:, c * 2 + cc, :], in_=kT[:, cc, c * 128:(c + 1) * 128],
                                                     identity=ident[:]) for c in range(NSUB) for cc in range(2)],
                  reads=[kTb, identb], writes=[PTb])
            for c in range(NSUB):
                tk.op("act", lambda a, c=c: a.activation(out=kp[:, c, :].rearrange("p (j d) -> p j d", j=2),
                                                         in_=PT[:, c * 2:c * 2 + 2, :], func=AF.Copy,
                                                         scale=beta[:, c, h:h + 1]),
                      reads=[PTb, betab], writes=[kpb])

        def state_update(h, c, cbi):
            for cc in range(2):
                tk.mm([lambda p, cc=cc: p.matmul(PK[cc][:, :], lhsT=kp[:, c, cc * 128:(cc + 1) * 128], rhs=vt[:, c, :],
                                                 start=True, stop=True)],
                      reads=[kpb, vtb[c]], writes=[PKb[cc]])
            tk.mm([lambda p, cc=cc: p.matmul(PD[:, 8 + cc:9 + cc], lhsT=kp[:, c, cc * 128:(cc + 1) * 128],
                                             rhs=onesb[:, 0:1], start=True, stop=True) for cc in range(2)],
                  reads=[kpb, onesbb], writes=[PDb])
            eg = egt[:, c, h:h + 1]
            eg16 = eg16t[:, c, h:h + 1]
            for cc in range(2):
                tk.op("dve", lambda v, cc=cc: v.tensor_tensor(out=Cst[:, h, cc, :], in0=PK[cc][:, :], in1=Cst[:, h, cc, :],
                                                              op=ALU.add),
                      reads=[PKb[cc]], writes=[Cstb[h][cc]])
                if cbi is not None:
                    tk.op("act", lambda a, cc=cc: a.activation(out=Cb[cbi][:, cc, :], in_=Cst[:, h, cc, :], func=AF.Copy,
                                                               scale=eg16),
                          reads=[Cstb[h][cc], egb], writes=[Cbb[cbi]])
            tk.op("dve", lambda v: v.tensor_tensor(out=nst[:, h, :], in0=nst[:, h, :], in1=PD[:, 8:10], op=ALU.add),
                  reads=[PDb], writes=[nstb[h]])
            if cbi is not None:
                tk.op("act", lambda a: a.activation(out=nbt[:, cbi, :], in_=nst[:, h, :], func=AF.Copy, scale=eg16),
                      reads=[nstb[h], egb], writes=[nbb[cbi]])
            for cc in range(2):
                tk.op("act", lambda a, cc=cc: a.activation(out=Cst[:, h, cc, :], in_=Cst[:, h, cc, :], func=AF.Copy, scale=eg),
                      reads=[egb], writes=[Cstb[h][cc]])
            tk.op("act", lambda a: a.activation(out=nst[:, h, :], in_=nst[:, h, :], func=AF.Copy, scale=eg),
                  reads=[egb], writes=[nstb[h]])

        def refresh_cb(h, cbi):
            tk.op("act", lambda a: a.activation(out=Cb[cbi][:, :, :], in_=Cst[:, h, :, :], func=AF.Copy, scale=1.0 / 16),
                  reads=Cstb[h], writes=[Cbb[cbi]])
            tk.op("act", lambda a: a.activation(out=nbt[:, cbi, :], in_=nst[:, h, :], func=AF.Copy, scale=1.0 / 16),
                  reads=[nstb[h]], writes=[nbb[cbi]])

        xsems = [tk.dsem() for _ in range(NSUB)]
        osems = [tk.dsem() for _ in range(NSUB)]

        def load_x(src, t):
            for s in range(NSUB):
                tk.dma("sp", xsems[s], Hs[:, s, :], src[t * T + s * 128:t * T + (s + 1) * 128, :], writes=[Hb[s]])

        def norm_with_v0(gain_d):
            wt_v, wb_v = wq.cur()
            banks = [(PJ[0], PJb[0]), (PJ[1], PJb[1]), (PN, PNb), (PS_, PSb)]
            rmsnorm_to_xnT(gain_d, after_sub=lambda s: proj_tm(wt_v, wb_v, s, pj=banks[s][0], pjb=banks[s][1]))
            for s in range(NSUB):
                tk.op("act", lambda a, s=s: a.activation(out=vt[:, s, :], in_=banks[s][0][:, :], func=AF.Copy),
                      reads=[banks[s][1]], writes=[vtb[s]] + ([xnb2b] if s >= 2 else []))
            wq.done()

        snap_l0 = [None]
        snap_l1 = [None]

        def prefix_tile(t):
            if t == 0:
                load_x(xp, 0)
            norm_with_v0(g0bc_d)
            if t + 1 < NT:
                load_x(xp, t + 1)
            else:
                load_x(xm, 0)
            gates_a()
            for h in range(H):
                def do_k():
                    wt, wb = wq.cur()
                    for cc in range(2):
                        conv_chunk(wt, wb, 256 + cc * 128, 16 + h * 2 + cc, kT, kTb, cc)
                    if h == 0:
                        gates_b()
                    if t == NT - 1:
                        for cc in range(2):
                            pj, pjb = proj_fm(wt, wb, cc * 128, n=3, t0=T - 3)
                            tk.op("act", lambda a, cc=cc, pj=pj: a.activation(out=halo[:, h * 2 + cc, 0:3], in_=pj[:, 0:3],
                                                                              func=AF.Copy),
                                  reads=[pjb], writes=[halob[h * 2 + cc]])
                    wq.done()

                def do_v(interleave):
                    wt, wb = wq.cur()
                    for s in range(NSUB):
                        pj, pjb = proj_tm(wt, wb, s)
                        tk.op("act", lambda a, s=s, pj=pj: a.activation(out=vt[:, s, :], in_=pj[:, :], func=AF.Copy),
                              reads=[pjb], writes=[vtb[s]])
                        if interleave:
                            if s == 1:
                                k_transposes(h)
                            if s >= 2:
                                state_update(h, s - 2, None)
                    wq.done()

                if h == 0:
                    do_k()
                    k_transposes(h)
                    for c in range(NSUB):
                        state_update(h, c, None)
                else:
                    do_k()
                    do_v(True)
                    state_update(h, NSUB - 2, None)
                    state_update(h, NSUB - 1, None)

        def layer0(t):
            norm_with_v0(g0bc_d)
            gates_a()
            for h in range(H):
                hi = h % 2
                tk.dma("sp", hghs[hi], hgh[hi][:], hgbc_d[:, h * 512:(h + 1) * 512], writes=[hghb[hi]])
                def do_qk():
                    wt, wb = wq.cur()
                    for cc in range(2):
                        conv_chunk(wt, wb, 256 + cc * 128, 16 + h * 2 + cc, kT, kTb, cc)
                    if h == 0:
                        gates_b()
                    for cc in range(2):
                        conv_chunk(wt, wb, cc * 128, h * 2 + cc, qT, qTb, cc)
                    wq.done()

                def do_v():
                    wt, wb = wq.cur()
                    for s in range(NSUB):
                        pj, pjb = proj_tm(wt, wb, s)
                        tk.op("act", lambda a, s=s, pj=pj: a.activation(out=vt[:, s, :], in_=pj[:, :], func=AF.Copy),
                              reads=[pjb], writes=[vtb[s]])
                    wq.done()

                if h == 0:
                    do_qk()
                    k_transposes(h)
                else:
                    wt, wb = wq.cur()
                    wt_v, wb_v = wq.peek(1)

                    def vproj(s):
                        pj, pjb = proj_tm(wt_v, wb_v, s)
                        tk.op("act", lambda a: a.activation(out=vt[:, s, :], in_=pj[:, :], func=AF.Copy),
                              reads=[pjb], writes=[vtb[s]])

                    for cc in range(2):
                        conv_chunk(wt, wb, 256 + cc * 128, 16 + h * 2 + cc, kT, kTb, cc)
                    conv_chunk(wt, wb, 0, h * 2, qT, qTb, 0)
                    vproj(0)
                    conv_chunk(wt, wb, 128, h * 2 + 1, qT, qTb, 1)
                    wq.done()
                    vproj(1)
                    k_transposes(h)
                    vproj(2)
                    vproj(3)
                    wq.done()
                refresh_cb(h, 0)
                wt, wb = wq.cur()
                tk.op("dve", lambda v: v.memset(sm[:, 32:36], 0.0), writes=smb[32:36])
                for c in range(NSUB):
                    cbi = c % 2
                    sti = c % 2
                    cs_ = slice(c * 128, (c + 1) * 128)
                    early_o = (h == 0 and c == 0)
                    if early_o:
                        pj, pjb = proj_tm(wt, wb, c)
                        tk.op("act", lambda a, c=c, pj=pj: a.activation(out=so[:, c, :], in_=pj[:, :], func=AF.Sigmoid),
                              reads=[pjb], writes=[sob[c]])
                    tk.mm([lambda p, cc=cc, cs_=cs_: p.matmul(PS_[:, 0:128], lhsT=kT[:, cc, cs_], rhs=qT[:, cc, cs_],
                                                              start=(cc == 0), stop=(cc == 1)) for cc in range(2)],
                          reads=[kTb, qTb], writes=[PSb])
                    tk.op("dve", lambda v, c=c, sti=sti: v.scalar_tensor_tensor(out=STt[sti], in0=PS_[:, 0:128],
                                                                                scalar=beta16[:, c, h:h + 1], in1=maskT[:],
                                                                                op0=ALU.mult, op1=ALU.mult),
                          reads=[PSb, beta16b, maskTb], writes=[STb[sti]])
                    if not early_o:
                        pj, pjb = proj_tm(wt, wb, c)
                        tk.op("act", lambda a, c=c, pj=pj: a.activation(out=so[:, c, :], in_=pj[:, :], func=AF.Sigmoid),
                              reads=[pjb], writes=[sob[c]])
                    tk.mm([lambda p, cs_=cs_, cbi=cbi: p.matmul(PN[:, :], lhsT=qT[:, 0, cs_], rhs=Cb[cbi][:, 0, :], start=True, stop=False),
                           lambda p, cs_=cs_, cbi=cbi: p.matmul(PN[:, :], lhsT=qT[:, 1, cs_], rhs=Cb[cbi][:, 1, :], start=False, stop=False),
                           lambda p, c=c, sti=sti: p.matmul(PN[:, :], lhsT=STt[sti], rhs=vt[:, c, :], start=False, stop=True)],
                          reads=[qTb, Cbb[cbi], STb[sti], vtb[c]], writes=[PNb])
                    tk.mm([lambda p, cs_=cs_, cbi=cbi: p.matmul(PD[:, 0:1], lhsT=qT[:, 0, cs_], rhs=nbt[:, cbi, 0:1], start=True, stop=False),
                           lambda p, cs_=cs_, cbi=cbi: p.matmul(PD[:, 0:1], lhsT=qT[:, 1, cs_], rhs=nbt[:, cbi, 1:2], start=False, stop=False),
                           lambda p, sti=sti: p.matmul(PD[:, 0:1], lhsT=STt[sti], rhs=onesb[:, 0:1], start=False, stop=True)],
                          reads=[qTb, nbb[cbi], STb[sti], onesbb], writes=[PDb])
                    tk.op("act", lambda a: a.activation(out=sm[:, 30:31], in_=PD[:, 0:1], func=AF.Copy),
                          reads=[PDb], writes=[smb[30]])
                    state_update(h, c, (c + 1) % 2 if c < NSUB - 1 else None)
                    tk.op("dve", lambda v: v.scalar_tensor_tensor(out=sm[:, 24:25], in0=sm[:, 30:31], scalar=-1.0,
                                                                  in1=sm[:, 30:31], op0=ALU.mult, op1=ALU.max),
                          reads=[smb[30]], writes=[smb[24]])
                    tk.op("dve", lambda v, c=c: v.tensor_tensor(out=sm[:, 25:26], in0=sm[:, 24:25], in1=emb[:, c, h:h + 1],
                                                                op=ALU.max), reads=[smb[24], embb], writes=[smb[25]])
                    tk.op("dve", lambda v: v.reciprocal(out=sm[:, 26:27], in_=sm[:, 25:26]),
                          reads=[smb[25]], writes=[smb[26]])
                    tk.op("dve", lambda v, c=c: v.scalar_tensor_tensor(out=so[:, c, :], in0=PN[:, :], scalar=sm[:, 26:27],
                                                                       in1=so[:, c, :], op0=ALU.mult, op1=ALU.mult),
                          reads=[PNb, smb[26]], writes=[sob[c]])
                    tk.op("act", lambda a, c=c: a.activation(out=ybf[0], in_=so[:, c, :], func=AF.Square,
                                                             accum_out=sm[:, 32 + c:33 + c]),
                          reads=[sob[c]], writes=[ybfb[0], smb[32 + c]])
                tk.op("act", lambda a: a.activation(out=sm[:, 36:40], in_=sm[:, 32:36], func=AF.Ln,
                                                    scale=1.0 / 512, bias=EPS), reads=smb[32:36], writes=smb[36:40])
                tk.op("act", lambda a: a.activation(out=sm[:, 40:44], in_=sm[:, 36:40], func=AF.Exp, scale=-0.5),
                      reads=smb[36:40], writes=smb[40:44])
                for c in range(NSUB):
                    tk.op("dve", lambda v, c=c: v.scalar_tensor_tensor(out=so[:, c, :], in0=so[:, c, :], scalar=sm[:, 40 + c:41 + c],
                                                                       in1=hgh[hi][:], op0=ALU.mult, op1=ALU.mult),
                          reads=[smb[40 + c], hghb[hi]], writes=[sob[c]])
                wq.done()
                wt, wb = wq.cur()

                def y_transposes(c):
                    yi = c % 2
                    cs_ = slice(c * 128, (c + 1) * 128)
                    tk.mm([lambda p, ec=ec: p.transpose(out=PT[:, ec, :], in_=ybf[yi][:, ec * 128:(ec + 1) * 128],
                                                        identity=ident[:]) for ec in range(4)],
                          reads=[ybfb[yi], identb], writes=[PTb])
                    tk.op("act", lambda a: a.activation(out=YT[:, h * 4:(h + 1) * 4, cs_], in_=PT[:, 0:4, :], func=AF.Copy),
                          reads=[PTb], writes=YTb[h * 4:(h + 1) * 4])

                for c in range(NSUB):
                    pj, pjb = proj_tm(wt, wb, c)
                    tk.op("act", lambda a, c=c, pj=pj: a.activation(out=sz[:, c, :], in_=pj[:, :], func=AF.Silu),
                          reads=[pjb], writes=[szb])
                    yi = c % 2
                    tk.op("dve", lambda v, c=c, yi=yi: v.tensor_tensor(out=ybf[yi], in0=so[:, c, :], in1=sz[:, c, :], op=ALU.mult),
                          reads=[sob[c], szb], writes=[ybfb[yi]])
                    if c >= 1:
                        y_transposes(c - 1)
                wq.done()
                y_transposes(NSUB - 1)
            gain_res[0] = None
            ensure_gain(g1bc_d)
            snap_l0[0] = tk.snapshot()
            wout(True)

        def wout(first_layer):
            accs = [(PJ[0], PJb[0]), (PJ[1], PJb[1]), (PN, PNb), (PS_, PSb)]

            def evac(s, cb):
                pj, pjb = accs[s]
                tk.op("dve", lambda v: v.tensor_tensor(out=Hs[:, s, cb * 512:(cb + 1) * 512], in0=pj[:, :],
                                                       in1=Hs[:, s, cb * 512:(cb + 1) * 512], op=ALU.add),
                      reads=[pjb], writes=[Hb[s]])

            def mm(wt, wb, s, half):
                pj, pjb = accs[s]
                proj_tm(wt, wb, s, pj=pj, pjb=pjb, start=(half == 0), stop=(half == 1),
                        lhs=lambda kc: YT[:, half * KC + kc, s * 128:(s + 1) * 128],
                        lhsb=YTb[half * KC:(half + 1) * KC])

            for cb in range(3):
                for half in range(2):
                    wt, wb = wq.cur()
                    for s in range(NSUB):
                        mm(wt, wb, s, half)
                    wq.done()
                for s in range(NSUB):
                    evac(s, cb)
            wta, wba = wq.cur()
            wtb, wbb = wq.peek(1)
            for s in range(NSUB):
                mm(wta, wba, s, 0)
                mm(wtb, wbb, s, 1)
                evac(s, 3)
            wq.done()
            wq.done()

        def layer1(t):
            tk.barrier(tks=snap_l0[0])
            tk.op("dve", lambda v: v.memset(st1[:], 0.0), writes=[st1b])
            tk.op("dve", lambda v: v.memset(st2[:], 0.0), writes=[st2b])

            def v1proj(wt, wb, vb, s):
                pj, pjb = proj_tm(wt, wb, s)
                dst = YT[:, vb * 4:(vb + 1) * 4, s * 128:(s + 1) * 128]
                tk.op("act", lambda a: a.activation(
                    out=dst, in_=pj[:, :].rearrange("p (j d) -> p j d", j=4), func=AF.Gelu,
                    accum_out=st1[:, s, vb:vb + 1]),
                    reads=[pjb], writes=YTb[vb * 4:(vb + 1) * 4] + [st1b])
                tk.op("act", lambda a: a.activation(
                    out=junk1.rearrange("p (j d) -> p j d", j=4), in_=dst, func=AF.Square,
                    accum_out=st2[:, s, vb:vb + 1]),
                    reads=YTb[vb * 4:(vb + 1) * 4], writes=[junk1b, st2b])

            wt0, wb0 = wq.cur()
            rmsnorm_to_xnT(g1bc_d, after_sub=lambda s: v1proj(wt0, wb0, 0, s))
            wq.done()
            ensure_gain(gfbc_d)
            for vb in range(1, 8):
                wt, wb = wq.cur()
                for s in range(NSUB):
                    v1proj(wt, wb, vb, s)
                wq.done()
            tk.op("dve", lambda v: v.reduce_sum(out=lnm[:], in_=st1[:], axis=mybir.AxisListType.X),
                  reads=[st1b], writes=[lnmb])
            tk.op("dve", lambda v: v.reduce_sum(out=lnt[:], in_=st2[:], axis=mybir.AxisListType.X),
                  reads=[st2b], writes=[lntb])
            tk.op("dve", lambda v: v.tensor_scalar(out=lnm[:], in0=lnm[:], scalar1=1.0 / 4096, scalar2=None, op0=ALU.mult),
                  writes=[lnmb])
            tk.op("dve", lambda v: v.tensor_tensor(out=lnr[:], in0=lnm[:], in1=lnm[:], op=ALU.mult),
                  reads=[lnmb], writes=[lnrb])
            tk.op("dve", lambda v: v.scalar_tensor_tensor(out=lnr[:], in0=lnt[:], scalar=1.0 / 4096, in1=lnr[:],
                                                          op0=ALU.mult, op1=ALU.subtract),
                  reads=[lntb], writes=[lnrb])
            tk.op("act", lambda a: a.activation(out=lnr[:], in_=lnr[:], func=AF.Ln, bias=EPS), writes=[lnrb])
            tk.op("act", lambda a: a.activation(out=lnr[:], in_=lnr[:], func=AF.Exp, scale=-0.5), writes=[lnrb])
            for s in range(NSUB):
                tk.op("dve", lambda v, s=s: v.tensor_scalar(out=YT[:, :, s * 128:(s + 1) * 128],
                                                            in0=YT[:, :, s * 128:(s + 1) * 128],
                                                            scalar1=lnm[:, s:s + 1], scalar2=lnr[:, s:s + 1],
                                                            op0=ALU.subtract, op1=ALU.mult),
                      reads=[lnmb, lnrb], writes=YTb)
            for g in range(8):
                wt, wb = wq.cur()
                for dc in range(4):
                    pj, pjb = proj_fm(wt, wb, dc * 128)
                    tk.op("act", lambda a, dc=dc, pj=pj: a.activation(out=uT[:, dc, :], in_=pj[:, :], func=AF.Gelu),
                          reads=[pjb], writes=[uTb[dc]])
                wq.done()
                wt, wb = wq.cur()
                for dc in range(4):
                    Dc = g * 4 + dc
                    zi = dc % 2
                    pj, pjb = proj_fm(wt, wb, dc * 128)
                    tk.op("act", lambda a, pj=pj, zi=zi: a.activation(out=szT[zi], in_=pj[:, :], func=AF.Silu),
                          reads=[pjb], writes=[szTb[zi]])
                    tk.mm([lambda p, c=c, Dc=Dc: p.matmul(PD[:, c * 128:(c + 1) * 128], lhsT=YT[:, Dc, c * 128:(c + 1) * 128],
                                                          rhs=wsm[:, g, :], start=True, stop=True) for c in range(NSUB)],
                          reads=[YTb[Dc], wsmb], writes=[PDb])
                    tk.op("dve", lambda v, Dc=Dc, zi=zi: v.scalar_tensor_tensor(
                        out=sv[zi].rearrange("p (c t) -> p c t", c=4), in0=PD[:, :].rearrange("p (c t) -> p c t", c=4),
                        scalar=lng[:, Dc:Dc + 1], in1=bsbc[:, g, :].unsqueeze(1).to_broadcast([128, 4, 128]),
                        op0=ALU.mult, op1=ALU.add),
                        reads=[PDb, lngb, bsbcb], writes=[svb[zi]])
                    tk.op("dve", lambda v, Dc=Dc, zi=zi: v.scalar_tensor_tensor(
                        out=sv[zi].rearrange("p (c t) -> p c t", c=4),
                        in0=rsbc[:, g, :].unsqueeze(1).to_broadcast([128, 4, 128]),
                        scalar=lnb[:, Dc:Dc + 1], in1=sv[zi].rearrange("p (c t) -> p c t", c=4),
                        op0=ALU.mult, op1=ALU.add),
                        reads=[rsbcb, lnbb], writes=[svb[zi]])
                    tk.op("dve", lambda v, dc=dc, zi=zi: v.tensor_tensor(out=sv[zi], in0=sv[zi], in1=uT[:, dc, :], op=ALU.mult),
                          reads=[uTb[dc]], writes=[svb[zi]])
                    tk.op("dve", lambda v, Dc=Dc, zi=zi: v.tensor_tensor(out=YT[:, Dc, :], in0=sv[zi], in1=szT[zi], op=ALU.mult),
                          reads=[svb[zi], szTb[zi]], writes=[YTb[Dc]])
                wq.done()
            snap_l1[0] = tk.snapshot()
            wout(False)

        def final_norm(t):
            tk.barrier(tks=snap_l1[0])
            ensure_gain(gfbc_d)
            tk.op("dve", lambda v: v.memset(sm[:, 0:4], 0.0), writes=smb[0:4])
            for s in range(NSUB):
                tk.op("act", lambda a, s=s: a.activation(out=junkN, in_=Hs[:, s, :], func=AF.Square,
                                                         accum_out=sm[:, s:s + 1]),
                      reads=[Hb[s]], writes=[junkNb, smb[s]])
            tk.op("act", lambda a: a.activation(out=sm[:, 8:12], in_=sm[:, 0:4], func=AF.Ln, scale=1.0 / D, bias=EPS),
                  reads=smb[0:4], writes=smb[8:12])
            tk.op("act", lambda a: a.activation(out=sm[:, 16:20], in_=sm[:, 8:12], func=AF.Exp, scale=-0.5),
                  reads=smb[8:12], writes=smb[16:20])
            for s in range(NSUB):
                stg = YT[:, 8 * s:8 * s + 8, :].rearrange("p a b -> p (a b)").bitcast(F32)
                tk.op("dve", lambda v, s=s, stg=stg: v.scalar_tensor_tensor(out=stg, in0=Hs[:, s, :],
                                                                            scalar=sm[:, 16 + s:17 + s], in1=gbc,
                                                                            op0=ALU.mult, op1=ALU.mult),
                      reads=[Hb[s], smb[16 + s], gbcb], writes=YTb[8 * s:8 * s + 8])
            if t + 1 < NT:
                ensure_gain(g0bc_d)
                load_x(xm, t + 1)
            for s in range(NSUB):
                stg = YT[:, 8 * s:8 * s + 8, :].rearrange("p a b -> p (a b)").bitcast(F32)
                tk.dma("sp", osems[s], out_d[t * T + s * 128:t * T + (s + 1) * 128, :], stg, reads=YTb[8 * s:8 * s + 8])

        for t in range(NT):
            tk.barrier()
            prefix_tile(t)
        tk.barrier()
        for h in range(H):
            tk.op("dve", lambda v, h=h: v.tensor_scalar(out=Cst[:, h, :, :], in0=Cst[:, h, :, :], scalar1=flag[:, 0:1],
                                                        scalar2=None, op0=ALU.mult), reads=[flagb], writes=Cstb[h])
        tk.op("dve", lambda v: v.tensor_scalar(out=nst[:], in0=nst[:], scalar1=flag[:, 0:1], scalar2=None, op0=ALU.mult),
              reads=[flagb], writes=nstb)
        tk.op("dve", lambda v: v.tensor_scalar(out=halo[:], in0=halo[:], scalar1=flag[:, 0:1], scalar2=None, op0=ALU.mult),
              reads=[flagb], writes=halob)
        for t in range(NT):
            tk.barrier()
            layer0(t)
            layer1(t)
            final_norm(t)
        for os_ in osems:
            tk.wait("sp", (os_.sem, os_.cnt, os_.key))
        tk.barrier(("sp",))
        assert wq.nu == len(tiles), (wq.nu, len(tiles))
    return nc


_NC_CACHE = {}


def _prep_inputs(inputs):
    f = lambda a: np.ascontiguousarray(np.asarray(a, dtype=np.float32))
    x = f(inputs["x"])
    bc = lambda v: np.ascontiguousarray(np.broadcast_to(f(v).reshape(1, -1), (128, f(v).size)))
    cw = f(inputs["mlstm_conv_w"])[0]
    convw = np.ascontiguousarray(cw.reshape(4, 32, 128).transpose(2, 1, 0).reshape(128, 128))
    convb = np.ascontiguousarray(f(inputs["mlstm_conv_b"])[0].reshape(32, 128).T)
    lng = np.ascontiguousarray(f(inputs["gmlp_ln_g"])[0].reshape(32, 128).T)
    lnb = np.ascontiguousarray(f(inputs["gmlp_ln_b"])[0].reshape(32, 128).T)
    ws = f(inputs["gmlp_w_s"])[0]
    wsT = np.ascontiguousarray(ws.transpose(2, 0, 1).reshape(128, 8 * 128))
    bsbc = bc(f(inputs["gmlp_b_s"])[0].reshape(-1))
    maskT = np.triu(np.ones((128, 128), np.float32))
    ident = np.eye(128, dtype=np.float32)
    common = {
        "w_in0": f(inputs["mlstm_w_in"])[0], "w_out0": f(inputs["mlstm_w_out"])[0],
        "w_in1": f(inputs["gmlp_w_in"])[0], "w_out1": f(inputs["gmlp_w_out"])[0],
        "g0bc": bc(inputs["mlstm_norm_g"]), "g1bc": bc(inputs["gmlp_norm_g"]), "gfbc": bc(inputs["final_norm_g"]),
        "convw": convw, "convb": convb, "gateb": bc(inputs["mlstm_gate_b"]), "hgbc": bc(inputs["mlstm_head_g"]),
        "lng": lng, "lnb": lnb, "wsT": wsT, "bsbc": bsbc, "maskT": maskT, "ident": ident,
    }
    in_maps = []
    for c in range(8):
        b, s = c // 2, c % 2
        m = dict(common)
        m["xm"] = np.ascontiguousarray(x[b, s * NTOK:(s + 1) * NTOK])
        m["xp"] = np.ascontiguousarray(x[b, 0:NTOK])
        m["flag"] = np.full((128, 1), float(s), np.float32)
        in_maps.append(m)
    return in_maps


def kernel(**inputs):
    in_maps = _prep_inputs(inputs)
    if "nc" not in _NC_CACHE:
        _NC_CACHE["nc"] = build_program()
    nc = _NC_CACHE["nc"]
    res = run_bass_kernel_spmd(nc, in_maps, core_ids=list(range(8)))
    out = np.empty((4, 4096, D), np.float32)
    for c in range(8):
        b, s = c // 2, c % 2
        out[b, s * NTOK:(s + 1) * NTOK] = res.results[c]["out"]
    return out
```

```python
import numpy as np
from contextlib import ExitStack
import concourse.bass as bass
import concourse.mybir as mybir
from concourse.bass_utils import run_bass_kernel_spmd

F32 = mybir.dt.float32
BF16 = mybir.dt.bfloat16
AF = mybir.ActivationFunctionType
ALU = mybir.AluOpType

D = 2048
NTOK = 2048
T = 512
NSUB = T // 128
NT = NTOK // T
KC = D // 128
H = 8
EPS = 1e-6
IN0 = 16400
IN1 = 12288
SAME_ENGINE_SYNC = True
NWSLOT = 3


class Buf:
    __slots__ = ("w", "r", "name")

    def __init__(self, name=""):
        self.w = None
        self.r = {}
        self.name = name


class DSem:
    def __init__(self, sem, key):
        self.sem = sem
        self.cnt = 0
        self.key = key


class TK:
    def __init__(self, nc, es):
        self.nc = nc
        self.es = es
        self.eng = {"pe": nc.tensor, "act": nc.scalar, "dve": nc.vector,
                    "pool": nc.gpsimd, "sp": nc.sync}
        self.sem = {k: es.enter_context(nc.semaphore("s_" + k)) for k in self.eng}
        self.cnt = {k: 0 for k in self.eng}
        self.seen = {k: {} for k in self.eng}
        self.ndsem = 0

    def dsem(self):
        self.ndsem += 1
        key = "d%d" % self.ndsem
        return DSem(self.es.enter_context(self.nc.semaphore(key)), key)

    def wait(self, e, tk):
        if tk is None:
            return
        sem, val, key = tk
        if key == e and (e == "pe" or not SAME_ENGINE_SYNC):
            return
        if self.seen[e].get(key, 0) >= val:
            return
        self.eng[e].wait_ge(sem, val)
        self.seen[e][key] = val

    def _deps(self, e, reads, writes):
        for b in reads:
            self.wait(e, b.w)
        for b in writes:
            self.wait(e, b.w)
            for t in b.r.values():
                self.wait(e, t)

    def _mark(self, tk, reads, writes):
        for b in reads:
            b.r[tk[2]] = tk
        for b in writes:
            b.w = tk
            b.r = {}

    def op(self, e, fn, reads=(), writes=()):
        self._deps(e, reads, writes)
        ins = fn(self.eng[e])
        self.cnt[e] += 1
        ins.then_inc(self.sem[e], 1)
        tk = (self.sem[e], self.cnt[e], e)
        self._mark(tk, reads, writes)
        return tk

    def mm(self, fns, reads=(), writes=()):
        e = "pe"
        self._deps(e, reads, writes)
        ins = None
        for fn in fns:
            ins = fn(self.eng[e])
        self.cnt[e] += 1
        ins.then_inc(self.sem[e], 1)
        tk = (self.sem[e], self.cnt[e], e)
        self._mark(tk, reads, writes)
        return tk

    def dma(self, e, ds, out, in_, reads=(), writes=()):
        self._deps(e, reads, writes)
        self.eng[e].dma_start(out=out, in_=in_).then_inc(ds.sem, 16)
        ds.cnt += 16
        tk = (ds.sem, ds.cnt, ds.key)
        self._mark(tk, reads, writes)
        return tk

    def snapshot(self):
        return [(self.sem[e], self.cnt[e], e) for e in ("pe", "act", "dve") if self.cnt[e] > 0]

    def barrier(self, engines=("pe", "act", "dve", "sp"), tks=None):
        if tks is None:
            tks = self.snapshot()
        for f in engines:
            for tk in tks:
                if tk[2] != f:
                    self.wait(f, tk)


class WQ:
    def __init__(self, tk, slots, tiles):
        self.tk = tk
        self.slots = slots
        self.tiles = tiles
        self.nl = 0
        self.nu = 0

    def _load(self):
        i = self.nl
        if i >= len(self.tiles):
            return
        t, b, ds = self.slots[i % len(self.slots)]
        tk = self.tk
        tk._deps("pool", (), [b])
        for (co, ncol, k0, nk, src) in self.tiles[i]:
            tk.eng["pool"].dma_start(out=t[:, k0:k0 + nk, co:co + ncol], in_=src).then_inc(ds.sem, 16)
            ds.cnt += 16
        tk._mark((ds.sem, ds.cnt, ds.key), (), [b])
        self.nl += 1

    def prime(self):
        for _ in self.slots:
            self._load()

    def cur(self):
        assert self.nu < len(self.tiles), "weight stream exhausted"
        t, b, ds = self.slots[self.nu % len(self.slots)]
        return t, b

    def peek(self, k):
        assert self.nu + k < self.nl, "tile not prefetched yet"
        t, b, ds = self.slots[(self.nu + k) % len(self.slots)]
        return t, b

    def done(self):
        self.nu += 1
        self._load()


def build_program():
    nc = bass.Bass("TRN2", target_bir_lowering=False)
    dt = lambda n, s, k="ExternalInput": nc.dram_tensor(n, s, F32, kind=k).ap()
    xm = dt("xm", [NTOK, D])
    xp = dt("xp", [NTOK, D])
    flag_d = dt("flag", [128, 1])
    w_in0 = dt("w_in0", [D, IN0])
    w_out0 = dt("w_out0", [4096, D])
    w_in1 = dt("w_in1", [D, IN1])
    w_out1 = dt("w_out1", [4096, D])
    g0bc_d = dt("g0bc", [128, D])
    g1bc_d = dt("g1bc", [128, D])
    gfbc_d = dt("gfbc", [128, D])
    convw_d = dt("convw", [128, 32 * 4])
    convb_d = dt("convb", [128, 32])
    gateb_d = dt("gateb", [128, 16])
    hgbc_d = dt("hgbc", [128, 4096])
    lng_d = dt("lng", [128, 32])
    lnb_d = dt("lnb", [128, 32])
    wsT_d = dt("wsT", [128, 8 * 128])
    bsbc_d = dt("bsbc", [128, 8 * 128])
    maskT_d = dt("maskT", [128, 128])
    ident_d = dt("ident", [128, 128])
    out_d = dt("out", [NTOK, D], "ExternalOutput")

    w_in0v = w_in0.rearrange("(kc p) c -> p kc c", p=128)
    w_in1v = w_in1.rearrange("(kc p) c -> p kc c", p=128)
    w_out0v = w_out0.rearrange("(kc p) c -> p kc c", p=128)
    w_out1v = w_out1.rearrange("(kc p) c -> p kc c", p=128)

    with ExitStack() as es:
        tk = TK(nc, es)
        sb = lambda n, s, d: es.enter_context(nc.sbuf_tensor("sb_" + n, s, d))
        ps = lambda n, s, d: es.enter_context(nc.psum_tensor("ps_" + n, s, d))

        Hs = sb("Hs", [128, NSUB, D], F32)
        Hb = [Buf("H%d" % i) for i in range(NSUB)]
        xnT = sb("xnT", [128, KC, T], BF16)
        xnTb = [Buf("xnT%d" % i) for i in range(NSUB)]
        YT = sb("YT", [128, 32, T], BF16)
        YTb = [Buf("YT%d" % i) for i in range(32)]
        Cst = sb("Cst", [128, H, 2, 512], F32)
        Cstb = [[Buf("C%d_%d" % (i, j)) for j in range(2)] for i in range(H)]
        nst = sb("nst", [128, H, 2], F32)
        nstb = [Buf("n%d" % i) for i in range(H)]
        wslots = []
        for i in range(NWSLOT):
            wslots.append((sb("w%d" % i, [128, KC, 512], BF16), Buf("w%d" % i), tk.dsem()))
        AB = sb("AB", [128, 12544], BF16)
        AFt = sb("AFt", [128, 1540], F32)
        xnb = AB[:, 6144:8192]
        xnbb = Buf("xnb")
        wg = sb("wg", [128, KC, 16], BF16); wgb = Buf()
        flag = sb("flagt", [128, 1], F32); flagb = Buf()
        convw = sb("convw", [128, 32, 4], F32); convwb = Buf()
        convb = sb("convb", [128, 32], F32); convbb = Buf()
        gateb = sb("gateb", [128, 16], F32); gatebb = Buf()
        lng = sb("lng", [128, 32], F32); lngb = Buf()
        lnb = sb("lnb", [128, 32], F32); lnbb = Buf()
        wsf = AFt[:, 0:1024].rearrange("p (g t) -> p g t", g=8); wsfb = Buf()
        wsm = sb("wsm", [128, 8, 128], BF16); wsmb = Buf()
        bsbc = sb("bsbc", [128, 8, 128], F32); bsbcb = Buf()
        rsbc = sb("rsbc", [128, 8, 128], F32); rsbcb = Buf()
        maskT = sb("maskT", [128, 128], F32); maskTb = Buf()
        identf = AFt[:, 1028:1156]; identfb = Buf()
        ident = sb("ident", [128, 128], BF16); identb = Buf()
        onesf = sb("onesf", [128, 128], F32); onesfb = Buf()
        onesb = sb("onesb", [128, 2], BF16); onesbb = Buf()
        hgh1 = sb("hgh", [128, 512], F32)
        hgh = [hgh1, hgh1]
        hghb1 = Buf()
        hghb = [hghb1, hghb1]
        hghs1 = tk.dsem()
        hghs = [hghs1, hghs1]
        halo = sb("halo", [128, 32, 4], F32); halob = [Buf() for _ in range(32)]
        gI = sb("gI", [128, NSUB, 8], F32); gIb = Buf()
        gZ = sb("gZ", [128, NSUB, 8], F32); gZb = Buf()
        gA = sb("gA", [128, NSUB, 8], F32); gAb = Buf()
        gL = sb("gL", [128, NSUB, 8], F32); gLb = Buf()
        beta = sb("beta", [128, NSUB, 8], F32); betab = Buf()
        beta16 = sb("beta16", [128, NSUB, 8], F32); beta16b = Buf()
        emb = sb("emb", [128, NSUB, 8], F32); embb = Buf()
        egt = sb("egt", [128, NSUB, 8], F32); egb = Buf()
        eg16t = sb("eg16t", [128, NSUB, 8], F32)
        sm = sb("sm", [128, 64], F32)
        smb = [Buf() for _ in range(64)]
        nbt = sb("nbt", [128, 2, 2], BF16); nbb = [Buf(), Buf()]
        st1 = sb("st1", [128, NSUB, 8], F32); st1b = Buf()
        st2 = sb("st2", [128, NSUB, 8], F32); st2b = Buf()
        lnm = sb("lnm", [128, NSUB], F32); lnmb = Buf()
        lnr = sb("lnr", [128, NSUB], F32); lnrb = Buf()
        lnt = sb("lnt", [128, NSUB], F32); lntb = Buf()

        def ab(o, n):
            return AB[:, o:o + n]
        qT = ab(0, 1024).rearrange("p (c t) -> p c t", c=2); qTb = Buf("qT")
        kT = ab(1024, 1024).rearrange("p (c t) -> p c t", c=2); kTb = Buf("kT")
        kp = ab(2048, 1024).rearrange("p (c d) -> p c d", c=4); kpb = Buf("kp")
        vt = ab(3072, 2048).rearrange("p (c e) -> p c e", c=4); vtb = [Buf("v%d" % i) for i in range(4)]
        so = ab(5120, 2048).rearrange("p (c e) -> p c e", c=4); sob = [Buf("so%d" % i) for i in range(4)]
        sz = ab(7168, 2048).rearrange("p (c e) -> p c e", c=4); szb = Buf("sz")
        STt = [ab(9216, 128), ab(9344, 128)]; STb = [Buf(), Buf()]
        ybf = [ab(9472, 512), ab(9984, 512)]; ybfb = [Buf(), Buf()]
        Cb = [ab(10496, 1024).rearrange("p (c e) -> p c e", c=2),
              ab(11520, 1024).rearrange("p (c e) -> p c e", c=2)]
        Cbb = [Buf(), Buf()]
        cbuf = AFt[:, 0:516]; cbufb = Buf("cbuf")
        acc = AFt[:, 516:1028]; accb = Buf("acc")
        hc = AFt[:, 1028:1540]; hcb = Buf("hc")
        gbc = AB[:, 8192:12288].bitcast(F32); gbcb = Buf("gbc"); gbcs = tk.dsem()
        xnb2 = ab(4096, 2048); xnb2b = Buf("xnb2")
        junkN = ab(0, 2048); junkNb = Buf("junkN")
        uT = ab(0, 2048).rearrange("p (c t) -> p c t", c=4); uTb = [Buf() for _ in range(4)]
        szT = [ab(2048, 512), ab(2560, 512)]; szTb = [Buf(), Buf()]
        junk1 = ab(3072, 512); junk1b = Buf()
        sv = [AFt[:, 0:512], AFt[:, 512:1024]]; svb = [Buf(), Buf()]

        PJ = [ps("PJ0", [128, 512], F32), ps("PJ1", [128, 512], F32)]
        PJb = [Buf("PJ0"), Buf("PJ1")]
        PT = ps("PT", [128, 8, 128], BF16); PTb = Buf("PT")
        PS_ = ps("PS", [128, 512], F32); PSb = Buf("PS")
        PN = ps("PN", [128, 512], F32); PNb = Buf("PN")
        PD = ps("PD", [128, 512], F32); PDb = Buf("PD"); PDnb = Buf("PDn")
        PK = [ps("PK0", [128, 512], F32), ps("PK1", [128, 512], F32)]
        PKb = [Buf("PK0"), Buf("PK1")]
        PT2 = PK[0][:, :].bitcast(BF16).rearrange("p (j d) -> p j d", j=8)
        PT3 = PK[1][:, :].bitcast(BF16).rearrange("p (j d) -> p j d", j=8)
        PT4 = PD[:, :].bitcast(BF16).rearrange("p (j d) -> p j d", j=8)
        pjc = [0]

        def next_pj():
            i = pjc[0] % 2
            pjc[0] += 1
            return PJ[i], PJb[i]

        tiles = []

        def qk_tile(h):
            return [(0, 256, 0, KC, w_in0v[:, :, h * 256:(h + 1) * 256]),
                    (256, 256, 0, KC, w_in0v[:, :, 2048 + h * 256:2048 + (h + 1) * 256])]

        def col_tile(wv, c0, n=512):
            return [(0, n, 0, KC, wv[:, :, c0:c0 + n])]

        def wout_tiles(wv):
            r = []
            for cb in range(4):
                for half in range(2):
                    r.append([(0, 512, 0, KC, wv[:, half * KC:(half + 1) * KC, cb * 512:(cb + 1) * 512])])
            return r

        for _pt in range(NT):
            for h in range(H):
                qk_ = qk_tile(h) if _pt == NT - 1 else qk_tile(h)[1:]
                v_ = col_tile(w_in0v, 4096 + h * 512)
                tiles += [v_, qk_] if h == 0 else [qk_, v_]
        for _mt in range(NT):
            for h in range(H):
                qk_ = qk_tile(h)
                v_ = col_tile(w_in0v, 4096 + h * 512)
                tiles += [v_, qk_] if h == 0 else [qk_, v_]
                tiles.append(col_tile(w_in0v, 8192 + h * 512))
                tiles.append(col_tile(w_in0v, 12288 + h * 512))
            tiles += wout_tiles(w_out0v)
            for vb in range(8):
                tiles.append(col_tile(w_in1v, 4096 + vb * 512))
            for g in range(8):
                tiles.append(col_tile(w_in1v, g * 512))
                tiles.append(col_tile(w_in1v, 8192 + g * 512))
            tiles += wout_tiles(w_out1v)
        wq = WQ(tk, wslots, tiles)

        cs = tk.dsem()

        cbufs = []

        def cload(t_ap, d_ap, b, e="sp"):
            tk.dma(e, cs, t_ap, d_ap, writes=[b])
            cbufs.append(b)

        cload(flag[:], flag_d, flagb)
        cload(convw[:], convw_d.rearrange("p (c j) -> p c j", j=4), convwb)
        cload(convb[:], convb_d, convbb)
        cload(gateb[:], gateb_d, gatebb)
        cload(lng[:], lng_d, lngb)
        cload(lnb[:], lnb_d, lnbb)
        cload(wsf, wsT_d.rearrange("p (g t) -> p g t", g=8), wsfb)
        cbufs += [cbufb, accb, hcb]
        cload(bsbc[:], bsbc_d.rearrange("p (g t) -> p g t", g=8), bsbcb)
        cload(maskT[:], maskT_d, maskTb)
        cload(identf, ident_d, identfb)
        for b_ in cbufs:
            b_.w = (cs.sem, cs.cnt, cs.key)
        wgs = tk.dsem()
        tk.dma("pool", wgs, wg[:], w_in0v[:, :, 16384:16400], writes=[wgb])
        wq.prime()

        tk.op("dve", lambda v: v.memset(onesf[:], 1.0), writes=[onesfb])
        tk.op("dve", lambda v: v.memset(onesb[:], 1.0), writes=[onesbb])
        tk.op("dve", lambda v: v.memset(halo[:], 0.0), writes=halob)
        tk.op("dve", lambda v: v.memset(Cst[:], 0.0), writes=[b for hb in Cstb for b in hb])
        tk.op("dve", lambda v: v.memset(nst[:], 0.0), writes=nstb)
        tk.op("act", lambda a: a.activation(out=ident[:], in_=identf, func=AF.Copy),
              reads=[identfb, hcb], writes=[identb])
        tk.op("dve", lambda v: v.tensor_tensor(out=wsm[:], in0=wsf,
                                               in1=maskT[:].unsqueeze(1).to_broadcast([128, 8, 128]),
                                               op=ALU.mult),
              reads=[wsfb, maskTb, cbufb, accb], writes=[wsmb])
        tk.op("dve", lambda v: v.tensor_tensor(out=wsf, in0=wsf,
                                               in1=maskT[:].unsqueeze(1).to_broadcast([128, 8, 128]),
                                               op=ALU.mult),
              reads=[maskTb], writes=[wsfb, cbufb, accb])
        for half in range(2):
            tk.mm([lambda p, half=half: p.matmul(PN[:, :], lhsT=onesf[:],
                                                 rhs=wsf[:, half * 4:(half + 1) * 4, :].rearrange("p g t -> p (g t)"),
                                                 start=True, stop=True)],
                  reads=[onesfb, wsfb, cbufb, accb], writes=[PNb])
            tk.op("act", lambda a, half=half: a.activation(
                out=rsbc[:, half * 4:(half + 1) * 4, :].rearrange("p g t -> p (g t)"), in_=PN[:, :], func=AF.Copy),
                reads=[PNb], writes=[rsbcb])

        gain_res = [None]

        def ensure_gain(gain_d):
            if gain_res[0] is gain_d:
                return
            tk.dma("sp", gbcs, gbc, gain_d, writes=[gbcb, szb] + STb + ybfb + Cbb)
            gain_res[0] = gain_d

        def rmsnorm_to_xnT(gain_d, after_sub=None):
            ensure_gain(gain_d)
            tk.op("dve", lambda v: v.memset(sm[:, 0:4], 0.0), writes=smb[0:4])
            tk.op("act", lambda a: a.activation(out=sm[:, 50:51], in_=onesf[:, 0:1], func=AF.Ln),
                  reads=[onesfb], writes=[smb[50]])

            def sq(s):
                tk.op("act", lambda a: a.activation(out=junkN, in_=Hs[:, s, :], func=AF.Square, accum_out=sm[:, s:s + 1]),
                      reads=[Hb[s]], writes=[junkNb, smb[s]])

            def lnexp(s):
                tk.op("act", lambda a: a.activation(out=sm[:, 8 + s:9 + s], in_=sm[:, s:s + 1], func=AF.Ln,
                                                    scale=1.0 / D, bias=EPS), reads=[smb[s]], writes=[smb[8 + s]])
                tk.op("act", lambda a: a.activation(out=sm[:, 16 + s:17 + s], in_=sm[:, 8 + s:9 + s], func=AF.Exp,
                                                    scale=-0.5), reads=[smb[8 + s]], writes=[smb[16 + s]])

            sq(0); lnexp(0); sq(1); lnexp(1)
            xb2 = [(xnb, xnbb), (xnb2, xnb2b)]
            ptv = [(PT[:, :, :], PTb), (PT2, PKb[0]), (PT3, PKb[1]), (PT4, PDb)]

            def scale(s):
                xb, xbb = xb2[s % 2]
                tk.op("dve", lambda v: v.scalar_tensor_tensor(out=xb, in0=Hs[:, s, :], scalar=sm[:, 16 + s:17 + s],
                                                              in1=gbc, op0=ALU.mult, op1=ALU.mult),
                      reads=[Hb[s], smb[16 + s], gbcb], writes=[xbb])

            scale(0)
            for s in range(NSUB):
                xb, xbb = xb2[s % 2]
                if s + 1 < NSUB:
                    scale(s + 1)
                for half in range(2):
                    pt, ptb = ptv[(s % 2) * 2 + half]
                    tk.mm([lambda p, j=j, half=half, xb=xb, pt=pt: p.transpose(
                        out=pt[:, j, :], in_=xb[:, (half * 8 + j) * 128:(half * 8 + j + 1) * 128],
                        identity=ident[:]) for j in range(8)],
                        reads=[xbb, identb], writes=[ptb])
                    if half == 0:
                        tk.op("act", lambda a, s=s, half=half, pt=pt: a.activation(
                            out=xnT[:, half * 8:(half + 1) * 8, s * 128:(s + 1) * 128], in_=pt, func=AF.Copy),
                            reads=[ptb], writes=[xnTb[s]])
                    else:
                        tk.op("dve", lambda v, s=s, half=half, pt=pt: v.tensor_copy(
                            out=xnT[:, half * 8:(half + 1) * 8, s * 128:(s + 1) * 128], in_=pt),
                            reads=[ptb], writes=[xnTb[s]])
                if s + 2 < NSUB:
                    sq(s + 2); lnexp(s + 2)
                if after_sub is not None and s >= 1:
                    after_sub(s - 1)
            if after_sub is not None:
                after_sub(NSUB - 1)

        def proj_fm(wt, wb, c0, n=T, t0=0):
            pj, pjb = next_pj()
            tk.mm([lambda p, kc=kc: p.matmul(pj[:, 0:n], lhsT=wt[:, kc, c0:c0 + 128], rhs=xnT[:, kc, t0:t0 + n],
                                             start=(kc == 0), stop=(kc == KC - 1)) for kc in range(KC)],
                  reads=[wb] + xnTb, writes=[pjb])
            return pj, pjb

        def proj_tm(wt, wb, s, n=512, pj=None, pjb=None, start=True, stop=True, lhs=None, lhsb=None, kcs=KC):
            if pj is None:
                pj, pjb = next_pj()
            if lhs is None:
                fns = [lambda p, kc=kc: p.matmul(pj[:, 0:n], lhsT=xnT[:, kc, s * 128:(s + 1) * 128], rhs=wt[:, kc, 0:n],
                                                 start=(start and kc == 0), stop=(stop and kc == kcs - 1))
                       for kc in range(kcs)]
                rd = [wb, xnTb[s]]
            else:
                fns = [lambda p, kc=kc: p.matmul(pj[:, 0:n], lhsT=lhs(kc), rhs=wt[:, kc, 0:n],
                                                 start=(start and kc == 0), stop=(stop and kc == kcs - 1))
                       for kc in range(kcs)]
                rd = [wb] + lhsb
            tk.mm(fns, reads=rd, writes=[pjb])
            return pj, pjb

        def gates_a():
            tk.mm([lambda p, s=s, kc=kc: p.matmul(PS_[:, s * 16:(s + 1) * 16], lhsT=xnT[:, kc, s * 128:(s + 1) * 128],
                                                  rhs=wg[:, kc, :], start=(kc == 0), stop=(kc == KC - 1))
                   for s in range(NSUB) for kc in range(KC)],
                  reads=xnTb + [wgb], writes=[PSb])
            psv = PS_[:, 0:NSUB * 16].rearrange("p (s g) -> p s g", g=16)
            tk.op("dve", lambda v: v.tensor_tensor(out=gI[:], in0=psv[:, :, 0:8],
                                                   in1=gateb[:, 0:8].unsqueeze(1).to_broadcast([128, NSUB, 8]),
                                                   op=ALU.add), reads=[PSb, gatebb], writes=[gIb])
            tk.op("dve", lambda v: v.tensor_tensor(out=gZ[:], in0=psv[:, :, 8:16],
                                                   in1=gateb[:, 8:16].unsqueeze(1).to_broadcast([128, NSUB, 8]),
                                                   op=ALU.add), reads=[PSb, gatebb], writes=[gZb])
            tk.op("dve", lambda v: v.scalar_tensor_tensor(out=gA[:], in0=gZ[:], scalar=-1.0, in1=gZ[:],
                                                          op0=ALU.mult, op1=ALU.max), reads=[gZb], writes=[gAb])
            tk.op("act", lambda a: a.activation(out=gA[:], in_=gA[:], func=AF.Exp, scale=-1.0),
                  reads=[gAb], writes=[gAb])
            tk.op("act", lambda a: a.activation(out=gA[:], in_=gA[:], func=AF.Ln, bias=1.0),
                  reads=[gAb], writes=[gAb])
            tk.op("dve", lambda v: v.scalar_tensor_tensor(out=gL[:], in0=gZ[:], scalar=0.0, in1=gA[:],
                                                          op0=ALU.min, op1=ALU.subtract),
                  reads=[gZb, gAb], writes=[gLb])

        def gates_b():
            psv = PS_[:, 0:NSUB * 16].rearrange("p (s g) -> p s g", g=16)
            gLf = gL[:].rearrange("p s g -> p (s g)")
            tk.mm([lambda p: p.matmul(PS_[:, 64:96], lhsT=maskT[:], rhs=gLf, start=True, stop=True),
                   lambda p: p.matmul(PS_[:, 96:128], lhsT=onesf[:], rhs=gLf, start=True, stop=True)],
                  reads=[maskTb, onesfb, gLb], writes=[PSb])
            bps = PS_[:, 64:96].rearrange("p (s g) -> p s g", g=8)
            gps = PS_[:, 96:128].rearrange("p (s g) -> p s g", g=8)
            tk.op("dve", lambda v: v.tensor_tensor(out=beta[:], in0=gI[:], in1=bps, op=ALU.subtract),
                  reads=[gIb, PSb], writes=[betab])
            tk.op("act", lambda a: a.activation(out=beta[:], in_=beta[:], func=AF.Exp), reads=[betab], writes=[betab])
            tk.op("act", lambda a: a.activation(out=beta16[:], in_=beta[:], func=AF.Copy, scale=1.0 / 16),
                  reads=[betab], writes=[beta16b])
            tk.op("act", lambda a: a.activation(out=emb[:], in_=bps, func=AF.Exp, scale=-1.0),
                  reads=[PSb], writes=[embb])
            tk.op("act", lambda a: a.activation(out=egt[:], in_=gps, func=AF.Exp), reads=[PSb], writes=[egb])
            tk.op("act", lambda a: a.activation(out=eg16t[:], in_=egt[:], func=AF.Copy, scale=1.0 / 16), writes=[egb])

        def conv_chunk(wt, wb, c0, ch, dstT, dstb, cc, n=T):
            pj, pjb = proj_fm(wt, wb, c0)
            tk.op("act", lambda a: a.activation(out=cbuf[:, 0:3], in_=halo[:, ch, 0:3], func=AF.Copy),
                  reads=[halob[ch]], writes=[cbufb])
            tk.op("act", lambda a: a.activation(out=cbuf[:, 3:3 + T], in_=pj[:, 0:T], func=AF.Copy),
                  reads=[pjb], writes=[cbufb])
            tk.op("act", lambda a: a.activation(out=halo[:, ch, 0:3], in_=cbuf[:, T:T + 3], func=AF.Copy),
                  reads=[cbufb], writes=[halob[ch]])
            tk.op("dve", lambda v: v.tensor_scalar(out=acc, in0=cbuf[:, 3:3 + T], scalar1=convw[:, ch, 3:4],
                                                   scalar2=convb[:, ch:ch + 1], op0=ALU.mult, op1=ALU.add),
                  reads=[cbufb, convwb, convbb], writes=[accb])
            for j in (2, 1, 0):
                tk.op("dve", lambda v, j=j: v.scalar_tensor_tensor(out=acc, in0=cbuf[:, j:j + T],
                                                                    scalar=convw[:, ch, j:j + 1], in1=acc,
                                                                    op0=ALU.mult, op1=ALU.add),
                      reads=[cbufb, convwb], writes=[accb])
            tk.op("act", lambda a: a.activation(out=dstT[:, cc, :], in_=acc, func=AF.Silu),
                  reads=[accb], writes=[dstb])

        def k_transposes(h):
            tk.mm([lambda p, c=c, cc=cc: p.transpose(out=PT[:, c * 2 + cc, :], in_=kT[:, cc, c * 128:(c + 1) * 128],
                                                     identity=ident[:]) for c in range(NSUB) for cc in range(2)],
                  reads=[kTb, identb], writes=[PTb])
            for c in range(NSUB):
                tk.op("act", lambda a, c=c: a.activation(out=kp[:, c, :].rearrange("p (j d) -> p j d", j=2),
                                                         in_=PT[:, c * 2:c * 2 + 2, :], func=AF.Copy,
                                                         scale=beta[:, c, h:h + 1]),
                      reads=[PTb, betab], writes=[kpb])

        def state_update(h, c, cbi):
            for cc in range(2):
                tk.mm([lambda p, cc=cc: p.matmul(PK[cc][:, :], lhsT=kp[:, c, cc * 128:(cc + 1) * 128], rhs=vt[:, c, :],
                                                 start=True, stop=True)],
                      reads=[kpb, vtb[c]], writes=[PKb[cc]])
            tk.mm([lambda p, cc=cc: p.matmul(PS_[:, 256 + cc:257 + cc], lhsT=kp[:, c, cc * 128:(cc + 1) * 128],
                                             rhs=onesb[:, 0:1], start=True, stop=True) for cc in range(2)],
                  reads=[kpb, onesbb], writes=[PSb])
            eg = egt[:, c, h:h + 1]
            eg16 = eg16t[:, c, h:h + 1]
            for cc in range(2):
                tk.op("dve", lambda v, cc=cc: v.tensor_tensor(out=Cst[:, h, cc, :], in0=PK[cc][:, :], in1=Cst[:, h, cc, :],
                                                              op=ALU.add),
                      reads=[PKb[cc]], writes=[Cstb[h][cc]])
                if cbi is not None:
                    tk.op("act", lambda a, cc=cc: a.activation(out=Cb[cbi][:, cc, :], in_=Cst[:, h, cc, :], func=AF.Copy,
                                                               scale=eg16),
                          reads=[Cstb[h][cc], egb], writes=[Cbb[cbi]])
            tk.op("dve", lambda v: v.tensor_tensor(out=nst[:, h, :], in0=nst[:, h, :], in1=PS_[:, 256:258], op=ALU.add),
                  reads=[PSb], writes=[nstb[h]])
            if cbi is not None:
                tk.op("act", lambda a: a.activation(out=nbt[:, cbi, :], in_=nst[:, h, :], func=AF.Copy, scale=eg16),
                      reads=[nstb[h], egb], writes=[nbb[cbi]])
            for cc in range(2):
                tk.op("act", lambda a, cc=cc: a.activation(out=Cst[:, h, cc, :], in_=Cst[:, h, cc, :], func=AF.Copy, scale=eg),
                      reads=[egb], writes=[Cstb[h][cc]])
            tk.op("act", lambda a: a.activation(out=nst[:, h, :], in_=nst[:, h, :], func=AF.Copy, scale=eg),
                  reads=[egb], writes=[nstb[h]])

        def refresh_cb(h, cbi):
            tk.op("act", lambda a: a.activation(out=Cb[cbi][:, :, :], in_=Cst[:, h, :, :], func=AF.Copy, scale=1.0 / 16),
                  reads=Cstb[h], writes=[Cbb[cbi]])
            tk.op("act", lambda a: a.activation(out=nbt[:, cbi, :], in_=nst[:, h, :], func=AF.Copy, scale=1.0 / 16),
                  reads=[nstb[h]], writes=[nbb[cbi]])

        xsems = [tk.dsem() for _ in range(NSUB)]
        osems = [tk.dsem() for _ in range(NSUB)]

        def load_x(src, t):
            for s in range(NSUB):
                tk.dma("sp", xsems[s], Hs[:, s, :], src[t * T + s * 128:t * T + (s + 1) * 128, :], writes=[Hb[s]])

        def norm_with_v0(gain_d):
            wt_v, wb_v = wq.cur()
            banks = [(PJ[0], PJb[0]), (PJ[1], PJb[1]), (PN, PNb), (PS_, PSb)]
            rmsnorm_to_xnT(gain_d, after_sub=lambda s: proj_tm(wt_v, wb_v, s, pj=banks[s][0], pjb=banks[s][1]))
            for s in range(NSUB):
                tk.op("act", lambda a, s=s: a.activation(out=vt[:, s, :], in_=banks[s][0][:, :], func=AF.Copy),
                      reads=[banks[s][1]], writes=[vtb[s]] + ([xnb2b] if s >= 2 else []))
            wq.done()

        snap_l0 = [None]
        snap_l1 = [None]

        def prefix_tile(t):
            if t == 0:
                load_x(xp, 0)
            norm_with_v0(g0bc_d)
            if t + 1 < NT:
                load_x(xp, t + 1)
            else:
                load_x(xm, 0)
            gates_a()
            for h in range(H):
                def do_k():
                    wt, wb = wq.cur()
                    for cc in range(2):
                        conv_chunk(wt, wb, 256 + cc * 128, 16 + h * 2 + cc, kT, kTb, cc)
                    if h == 0:
                        gates_b()
                    if t == NT - 1:
                        for cc in range(2):
                            pj, pjb = proj_fm(wt, wb, cc * 128, n=3, t0=T - 3)
                            tk.op("act", lambda a, cc=cc, pj=pj: a.activation(out=halo[:, h * 2 + cc, 0:3], in_=pj[:, 0:3],
                                                                              func=AF.Copy),
                                  reads=[pjb], writes=[halob[h * 2 + cc]])
                    wq.done()

                def do_v(interleave):
                    wt, wb = wq.cur()
                    for s in range(NSUB):
                        pj, pjb = proj_tm(wt, wb, s)
                        tk.op("act", lambda a, s=s, pj=pj: a.activation(out=vt[:, s, :], in_=pj[:, :], func=AF.Copy),
                              reads=[pjb], writes=[vtb[s]])
                        if interleave:
                            if s == 1:
                                k_transposes(h)
                            if s >= 2:
                                state_update(h, s - 2, None)
                    wq.done()

                if h == 0:
                    do_k()
                    k_transposes(h)
                    for c in range(NSUB):
                        state_update(h, c, None)
                else:
                    do_k()
                    do_v(True)
                    state_update(h, NSUB - 2, None)
                    state_update(h, NSUB - 1, None)

        def layer0(t):
            norm_with_v0(g0bc_d)
            gates_a()
            for h in range(H):
                hi = h % 2
                tk.dma("sp", hghs[hi], hgh[hi][:], hgbc_d[:, h * 512:(h + 1) * 512], writes=[hghb[hi]])
                def do_qk():
                    wt, wb = wq.cur()
                    for cc in range(2):
                        conv_chunk(wt, wb, 256 + cc * 128, 16 + h * 2 + cc, kT, kTb, cc)
                    if h == 0:
                        gates_b()
                    for cc in range(2):
                        conv_chunk(wt, wb, cc * 128, h * 2 + cc, qT, qTb, cc)
                    wq.done()

                def do_v():
                    wt, wb = wq.cur()
                    for s in range(NSUB):
                        pj, pjb = proj_tm(wt, wb, s)
                        tk.op("act", lambda a, s=s, pj=pj: a.activation(out=vt[:, s, :], in_=pj[:, :], func=AF.Copy),
                              reads=[pjb], writes=[vtb[s]])
                    wq.done()

                if h == 0:
                    do_qk()
                    k_transposes(h)
                else:
                    wt, wb = wq.cur()
                    wt_v, wb_v = wq.peek(1)

                    def vproj(s):
                        pj, pjb = proj_tm(wt_v, wb_v, s)
                        tk.op("act", lambda a: a.activation(out=vt[:, s, :], in_=pj[:, :], func=AF.Copy),
                              reads=[pjb], writes=[vtb[s]])

                    for cc in range(2):
                        conv_chunk(wt, wb, 256 + cc * 128, 16 + h * 2 + cc, kT, kTb, cc)
                    conv_chunk(wt, wb, 0, h * 2, qT, qTb, 0)
                    vproj(0)
                    conv_chunk(wt, wb, 128, h * 2 + 1, qT, qTb, 1)
                    wq.done()
                    vproj(1)
                    k_transposes(h)
                    vproj(2)
                    vproj(3)
                    wq.done()
                refresh_cb(h, 0)
                wt, wb = wq.cur()
                tk.op("dve", lambda v: v.memset(sm[:, 32:36], 0.0), writes=smb[32:36])
                for c in range(NSUB):
                    cbi = c % 2
                    sti = c % 2
                    cs_ = slice(c * 128, (c + 1) * 128)
                    early_o = (h == 0 and c == 0)
                    if early_o:
                        pj, pjb = proj_tm(wt, wb, c)
                        tk.op("act", lambda a, c=c, pj=pj: a.activation(out=so[:, c, :], in_=pj[:, :], func=AF.Sigmoid),
                              reads=[pjb], writes=[sob[c]])
                    tk.mm([lambda p, cc=cc, cs_=cs_: p.matmul(PS_[:, 0:128], lhsT=kT[:, cc, cs_], rhs=qT[:, cc, cs_],
                                                              start=(cc == 0), stop=(cc == 1)) for cc in range(2)],
                          reads=[kTb, qTb], writes=[PSb])
                    tk.op("dve", lambda v, c=c, sti=sti: v.scalar_tensor_tensor(out=STt[sti], in0=PS_[:, 0:128],
                                                                                scalar=beta16[:, c, h:h + 1], in1=maskT[:],
                                                                                op0=ALU.mult, op1=ALU.mult),
                          reads=[PSb, beta16b, maskTb], writes=[STb[sti]])
                    if not early_o:
                        pj, pjb = proj_tm(wt, wb, c)
                        tk.op("act", lambda a, c=c, pj=pj: a.activation(out=so[:, c, :], in_=pj[:, :], func=AF.Sigmoid),
                              reads=[pjb], writes=[sob[c]])
                    tk.mm([lambda p, cs_=cs_, cbi=cbi: p.matmul(PN[:, :], lhsT=qT[:, 0, cs_], rhs=Cb[cbi][:, 0, :], start=True, stop=False),
                           lambda p, cs_=cs_, cbi=cbi: p.matmul(PN[:, :], lhsT=qT[:, 1, cs_], rhs=Cb[cbi][:, 1, :], start=False, stop=False),
                           lambda p, c=c, sti=sti: p.matmul(PN[:, :], lhsT=STt[sti], rhs=vt[:, c, :], start=False, stop=True)],
                          reads=[qTb, Cbb[cbi], STb[sti], vtb[c]], writes=[PNb])
                    tk.mm([lambda p, cs_=cs_, cbi=cbi: p.matmul(PD[:, 0:1], lhsT=qT[:, 0, cs_], rhs=nbt[:, cbi, 0:1], start=True, stop=False),
                           lambda p, cs_=cs_, cbi=cbi: p.matmul(PD[:, 0:1], lhsT=qT[:, 1, cs_], rhs=nbt[:, cbi, 1:2], start=False, stop=False),
                           lambda p, sti=sti: p.matmul(PD[:, 0:1], lhsT=STt[sti], rhs=onesb[:, 0:1], start=False, stop=True)],
                          reads=[qTb, nbb[cbi], STb[sti], onesbb], writes=[PDb])
                    tk.op("act", lambda a: a.activation(out=sm[:, 30:31], in_=PD[:, 0:1], func=AF.Copy),
                          reads=[PDb], writes=[smb[30]])
                    state_update(h, c, (c + 1) % 2 if c < NSUB - 1 else None)
                    tk.op("dve", lambda v: v.scalar_tensor_tensor(out=sm[:, 24:25], in0=sm[:, 30:31], scalar=-1.0,
                                                                  in1=sm[:, 30:31], op0=ALU.mult, op1=ALU.max),
                          reads=[smb[30]], writes=[smb[24]])
                    tk.op("dve", lambda v, c=c: v.tensor_tensor(out=sm[:, 25:26], in0=sm[:, 24:25], in1=emb[:, c, h:h + 1],
                                                                op=ALU.max), reads=[smb[24], embb], writes=[smb[25]])
                    tk.op("dve", lambda v: v.reciprocal(out=sm[:, 26:27], in_=sm[:, 25:26]),
                          reads=[smb[25]], writes=[smb[26]])
                    tk.op("dve", lambda v, c=c: v.scalar_tensor_tensor(out=so[:, c, :], in0=PN[:, :], scalar=sm[:, 26:27],
                                                                       in1=so[:, c, :], op0=ALU.mult, op1=ALU.mult),
                          reads=[PNb, smb[26]], writes=[sob[c]])
                    tk.op("act", lambda a, c=c: a.activation(out=ybf[0], in_=so[:, c, :], func=AF.Square,
                                                             accum_out=sm[:, 32 + c:33 + c]),
                          reads=[sob[c]], writes=[ybfb[0], smb[32 + c]])
                tk.op("act", lambda a: a.activation(out=sm[:, 36:40], in_=sm[:, 32:36], func=AF.Ln,
                                                    scale=1.0 / 512, bias=EPS), reads=smb[32:36], writes=smb[36:40])
                tk.op("act", lambda a: a.activation(out=sm[:, 40:44], in_=sm[:, 36:40], func=AF.Exp, scale=-0.5),
                      reads=smb[36:40], writes=smb[40:44])
                wq.done()
                wt, wb = wq.cur()

                def y_transposes(c):
                    yi = c % 2
                    cs_ = slice(c * 128, (c + 1) * 128)
                    tk.mm([lambda p, ec=ec: p.transpose(out=PT[:, ec, :], in_=ybf[yi][:, ec * 128:(ec + 1) * 128],
                                                        identity=ident[:]) for ec in range(4)],
                          reads=[ybfb[yi], identb], writes=[PTb])
                    tk.op("act", lambda a: a.activation(out=YT[:, h * 4:(h + 1) * 4, cs_], in_=PT[:, 0:4, :], func=AF.Copy),
                          reads=[PTb], writes=YTb[h * 4:(h + 1) * 4])

                for c in range(NSUB):
                    pj, pjb = proj_tm(wt, wb, c)
                    tk.op("act", lambda a, c=c, pj=pj: a.activation(out=sz[:, c, :], in_=pj[:, :], func=AF.Silu),
                          reads=[pjb], writes=[szb])
                    yi = c % 2
                    tk.op("dve", lambda v, c=c: v.scalar_tensor_tensor(out=hc, in0=so[:, c, :], scalar=sm[:, 40 + c:41 + c],
                                                                       in1=hgh[hi][:], op0=ALU.mult, op1=ALU.mult),
                          reads=[sob[c], smb[40 + c], hghb[hi]], writes=[hcb])
                    tk.op("dve", lambda v, c=c, yi=yi: v.tensor_tensor(out=ybf[yi], in0=hc, in1=sz[:, c, :], op=ALU.mult),
                          reads=[hcb, szb], writes=[ybfb[yi]])
                    if c >= 1:
                        y_transposes(c - 1)
                wq.done()
                y_transposes(NSUB - 1)
            gain_res[0] = None
            ensure_gain(g1bc_d)
            snap_l0[0] = tk.snapshot()
            wout(True)

        def wout(first_layer):
            accs = [(PJ[0], PJb[0]), (PJ[1], PJb[1]), (PN, PNb), (PS_, PSb)]

            def evac(s, cb):
                pj, pjb = accs[s]
                tk.op("dve", lambda v: v.tensor_tensor(out=Hs[:, s, cb * 512:(cb + 1) * 512], in0=pj[:, :],
                                                       in1=Hs[:, s, cb * 512:(cb + 1) * 512], op=ALU.add),
                      reads=[pjb], writes=[Hb[s]])

            def mm(wt, wb, s, half):
                pj, pjb = accs[s]
                proj_tm(wt, wb, s, pj=pj, pjb=pjb, start=(half == 0), stop=(half == 1),
                        lhs=lambda kc: YT[:, half * KC + kc, s * 128:(s + 1) * 128],
                        lhsb=YTb[half * KC:(half + 1) * KC])

            for cb in range(3):
                for half in range(2):
                    wt, wb = wq.cur()
                    for s in range(NSUB):
                        mm(wt, wb, s, half)
                    wq.done()
                for s in range(NSUB):
                    evac(s, cb)
            wta, wba = wq.cur()
            wtb, wbb = wq.peek(1)
            for s in range(NSUB):
                mm(wta, wba, s, 0)
                mm(wtb, wbb, s, 1)
                evac(s, 3)
            wq.done()
            wq.done()

        def layer1(t):
            tk.barrier(tks=snap_l0[0])
            tk.op("dve", lambda v: v.memset(st1[:], 0.0), writes=[st1b])
            tk.op("dve", lambda v: v.memset(st2[:], 0.0), writes=[st2b])

            def v1proj(wt, wb, vb, s):
                pj, pjb = proj_tm(wt, wb, s)
                dst = YT[:, vb * 4:(vb + 1) * 4, s * 128:(s + 1) * 128]
                tk.op("act", lambda a: a.activation(
                    out=dst, in_=pj[:, :].rearrange("p (j d) -> p j d", j=4), func=AF.Gelu,
                    accum_out=st1[:, s, vb:vb + 1]),
                    reads=[pjb], writes=YTb[vb * 4:(vb + 1) * 4] + [st1b])
                tk.op("act", lambda a: a.activation(
                    out=junk1.rearrange("p (j d) -> p j d", j=4), in_=dst, func=AF.Square,
                    accum_out=st2[:, s, vb:vb + 1]),
                    reads=YTb[vb * 4:(vb + 1) * 4], writes=[junk1b, st2b])

            wt0, wb0 = wq.cur()
            rmsnorm_to_xnT(g1bc_d, after_sub=lambda s: v1proj(wt0, wb0, 0, s))
            wq.done()
            ensure_gain(gfbc_d)
            for vb in range(1, 8):
                wt, wb = wq.cur()
                for s in range(NSUB):
                    v1proj(wt, wb, vb, s)
                wq.done()
            tk.op("dve", lambda v: v.reduce_sum(out=lnm[:], in_=st1[:], axis=mybir.AxisListType.X),
                  reads=[st1b], writes=[lnmb])
            tk.op("dve", lambda v: v.reduce_sum(out=lnt[:], in_=st2[:], axis=mybir.AxisListType.X),
                  reads=[st2b], writes=[lntb])
            tk.op("dve", lambda v: v.tensor_scalar(out=lnm[:], in0=lnm[:], scalar1=1.0 / 4096, scalar2=None, op0=ALU.mult),
                  writes=[lnmb])
            tk.op("dve", lambda v: v.tensor_tensor(out=lnr[:], in0=lnm[:], in1=lnm[:], op=ALU.mult),
                  reads=[lnmb], writes=[lnrb])
            tk.op("dve", lambda v: v.scalar_tensor_tensor(out=lnr[:], in0=lnt[:], scalar=1.0 / 4096, in1=lnr[:],
                                                          op0=ALU.mult, op1=ALU.subtract),
                  reads=[lntb], writes=[lnrb])
            tk.op("act", lambda a: a.activation(out=lnr[:], in_=lnr[:], func=AF.Ln, bias=EPS), writes=[lnrb])
            tk.op("act", lambda a: a.activation(out=lnr[:], in_=lnr[:], func=AF.Exp, scale=-0.5), writes=[lnrb])
            for s in range(NSUB):
                tk.op("dve", lambda v, s=s: v.tensor_scalar(out=YT[:, :, s * 128:(s + 1) * 128],
                                                            in0=YT[:, :, s * 128:(s + 1) * 128],
                                                            scalar1=lnm[:, s:s + 1], scalar2=lnr[:, s:s + 1],
                                                            op0=ALU.subtract, op1=ALU.mult),
                      reads=[lnmb, lnrb], writes=YTb)
            for g in range(8):
                wt, wb = wq.cur()
                for dc in range(4):
                    pj, pjb = proj_fm(wt, wb, dc * 128)
                    tk.op("act", lambda a, dc=dc, pj=pj: a.activation(out=uT[:, dc, :], in_=pj[:, :], func=AF.Gelu),
                          reads=[pjb], writes=[uTb[dc]])
                wq.done()
                wt, wb = wq.cur()
                for dc in range(4):
                    Dc = g * 4 + dc
                    zi = dc % 2
                    pj, pjb = proj_fm(wt, wb, dc * 128)
                    tk.op("act", lambda a, pj=pj, zi=zi: a.activation(out=szT[zi], in_=pj[:, :], func=AF.Silu),
                          reads=[pjb], writes=[szTb[zi]])
                    tk.mm([lambda p, c=c, Dc=Dc: p.matmul(PD[:, c * 128:(c + 1) * 128], lhsT=YT[:, Dc, c * 128:(c + 1) * 128],
                                                          rhs=wsm[:, g, :], start=True, stop=True) for c in range(NSUB)],
                          reads=[YTb[Dc], wsmb], writes=[PDb])
                    tk.op("dve", lambda v, Dc=Dc, zi=zi: v.scalar_tensor_tensor(
                        out=sv[zi].rearrange("p (c t) -> p c t", c=4), in0=PD[:, :].rearrange("p (c t) -> p c t", c=4),
                        scalar=lng[:, Dc:Dc + 1], in1=bsbc[:, g, :].unsqueeze(1).to_broadcast([128, 4, 128]),
                        op0=ALU.mult, op1=ALU.add),
                        reads=[PDb, lngb, bsbcb], writes=[svb[zi]])
                    tk.op("dve", lambda v, Dc=Dc, zi=zi: v.scalar_tensor_tensor(
                        out=sv[zi].rearrange("p (c t) -> p c t", c=4),
                        in0=rsbc[:, g, :].unsqueeze(1).to_broadcast([128, 4, 128]),
                        scalar=lnb[:, Dc:Dc + 1], in1=sv[zi].rearrange("p (c t) -> p c t", c=4),
                        op0=ALU.mult, op1=ALU.add),
                        reads=[rsbcb, lnbb], writes=[svb[zi]])
                    tk.op("dve", lambda v, dc=dc, zi=zi: v.tensor_tensor(out=sv[zi], in0=sv[zi], in1=uT[:, dc, :], op=ALU.mult),
                          reads=[uTb[dc]], writes=[svb[zi]])
                    tk.op("dve", lambda v, Dc=Dc, zi=zi: v.tensor_tensor(out=YT[:, Dc, :], in0=sv[zi], in1=szT[zi], op=ALU.mult),
                          reads=[svb[zi], szTb[zi]], writes=[YTb[Dc]])
                wq.done()
            snap_l1[0] = tk.snapshot()
            wout(False)

        def final_norm(t):
            tk.barrier(tks=snap_l1[0])
            ensure_gain(gfbc_d)
            tk.op("dve", lambda v: v.memset(sm[:, 0:4], 0.0), writes=smb[0:4])
            for s in range(NSUB):
                tk.op("act", lambda a, s=s: a.activation(out=junkN, in_=Hs[:, s, :], func=AF.Square,
                                                         accum_out=sm[:, s:s + 1]),
                      reads=[Hb[s]], writes=[junkNb, smb[s]])
            tk.op("act", lambda a: a.activation(out=sm[:, 8:12], in_=sm[:, 0:4], func=AF.Ln, scale=1.0 / D, bias=EPS),
                  reads=smb[0:4], writes=smb[8:12])
            tk.op("act", lambda a: a.activation(out=sm[:, 16:20], in_=sm[:, 8:12], func=AF.Exp, scale=-0.5),
                  reads=smb[8:12], writes=smb[16:20])
            for s in range(NSUB):
                stg = YT[:, 8 * s:8 * s + 8, :].rearrange("p a b -> p (a b)").bitcast(F32)
                tk.op("dve", lambda v, s=s, stg=stg: v.scalar_tensor_tensor(out=stg, in0=Hs[:, s, :],
                                                                            scalar=sm[:, 16 + s:17 + s], in1=gbc,
                                                                            op0=ALU.mult, op1=ALU.mult),
                      reads=[Hb[s], smb[16 + s], gbcb], writes=YTb[8 * s:8 * s + 8])
            if t + 1 < NT:
                ensure_gain(g0bc_d)
                load_x(xm, t + 1)
            for s in range(NSUB):
                stg = YT[:, 8 * s:8 * s + 8, :].rearrange("p a b -> p (a b)").bitcast(F32)
                tk.dma("sp", osems[s], out_d[t * T + s * 128:t * T + (s + 1) * 128, :], stg, reads=YTb[8 * s:8 * s + 8])

        for t in range(NT):
            tk.barrier()
            prefix_tile(t)
        tk.barrier()
        for h in range(H):
            tk.op("dve", lambda v, h=h: v.tensor_scalar(out=Cst[:, h, :, :], in0=Cst[:, h, :, :], scalar1=flag[:, 0:1],
                                                        scalar2=None, op0=ALU.mult), reads=[flagb], writes=Cstb[h])
        tk.op("dve", lambda v: v.tensor_scalar(out=nst[:], in0=nst[:], scalar1=flag[:, 0:1], scalar2=None, op0=ALU.mult),
              reads=[flagb], writes=nstb)
        tk.op("dve", lambda v: v.tensor_scalar(out=halo[:], in0=halo[:], scalar1=flag[:, 0:1], scalar2=None, op0=ALU.mult),
              reads=[flagb], writes=halob)
        for t in range(NT):
            tk.barrier()
            layer0(t)
            layer1(t)
            final_norm(t)
        for os_ in osems:
            tk.wait("sp", (os_.sem, os_.cnt, os_.key))
        tk.barrier(("sp",))
        assert wq.nu == len(tiles), (wq.nu, len(tiles))
    return nc


_NC_CACHE = {}


def _prep_inputs(inputs):
    f = lambda a: np.ascontiguousarray(np.asarray(a, dtype=np.float32))
    x = f(inputs["x"])
    bc = lambda v: np.ascontiguousarray(np.broadcast_to(f(v).reshape(1, -1), (128, f(v).size)))
    cw = f(inputs["mlstm_conv_w"])[0]
    convw = np.ascontiguousarray(cw.reshape(4, 32, 128).transpose(2, 1, 0).reshape(128, 128))
    convb = np.ascontiguousarray(f(inputs["mlstm_conv_b"])[0].reshape(32, 128).T)
    lng = np.ascontiguousarray(f(inputs["gmlp_ln_g"])[0].reshape(32, 128).T)
    lnb = np.ascontiguousarray(f(inputs["gmlp_ln_b"])[0].reshape(32, 128).T)
    ws = f(inputs["gmlp_w_s"])[0]
    wsT = np.ascontiguousarray(ws.transpose(2, 0, 1).reshape(128, 8 * 128))
    bsbc = bc(f(inputs["gmlp_b_s"])[0].reshape(-1))
    maskT = np.triu(np.ones((128, 128), np.float32))
    ident = np.eye(128, dtype=np.float32)
    common = {
        "w_in0": f(inputs["mlstm_w_in"])[0], "w_out0": f(inputs["mlstm_w_out"])[0],
        "w_in1": f(inputs["gmlp_w_in"])[0], "w_out1": f(inputs["gmlp_w_out"])[0],
        "g0bc": bc(inputs["mlstm_norm_g"]), "g1bc": bc(inputs["gmlp_norm_g"]), "gfbc": bc(inputs["final_norm_g"]),
        "convw": convw, "convb": convb, "gateb": bc(inputs["mlstm_gate_b"]), "hgbc": bc(inputs["mlstm_head_g"]),
        "lng": lng, "lnb": lnb, "wsT": wsT, "bsbc": bsbc, "maskT": maskT, "ident": ident,
    }
    in_maps = []
    for c in range(8):
        b, s = c // 2, c % 2
        m = dict(common)
        m["xm"] = np.ascontiguousarray(x[b, s * NTOK:(s + 1) * NTOK])
        m["xp"] = np.ascontiguousarray(x[b, 0:NTOK])
        m["flag"] = np.full((128, 1), float(s), np.float32)
        in_maps.append(m)
    return in_maps


def kernel(**inputs):
    in_maps = _prep_inputs(inputs)
    if "nc" not in _NC_CACHE:
        _NC_CACHE["nc"] = build_program()
    nc = _NC_CACHE["nc"]
    res = run_bass_kernel_spmd(nc, in_maps, core_ids=list(range(8)))
    out = np.empty((4, 4096, D), np.float32)
    for c in range(8):
        b, s = c // 2, c % 2
        out[b, s * NTOK:(s + 1) * NTOK] = res.results[c]["out"]
    return out
```
